# Optimizing a Trainium2 kernel written in Bass

```python
import jax, jax.numpy as jnp
from jax import lax
import numpy as np

D_MODEL = 1024
BATCH = 4
SEQ = 8192
DEPTH = 4

GRID_W = 64
CTX_LEN = 256
POOL_GROUPS = 4
POOL_WINDOWS = (2, 4, 8, 16)
POOL_WIDTH = D_MODEL // 4
FOURIER_GROUPS = 4
FOURIER_WIDTH = D_MODEL // 4
HEAD_DIM = 64
N_HEADS = (D_MODEL // 2) // HEAD_DIM
ATTN_WIDTH = N_HEADS * HEAD_DIM
MIX_WIDTH = POOL_WIDTH + FOURIER_WIDTH + ATTN_WIDTH
P_END = POOL_WIDTH
F_END = P_END + FOURIER_WIDTH
Q_END = F_END + ATTN_WIDTH
K_END = Q_END + ATTN_WIDTH
IN_WIDTH = K_END + ATTN_WIDTH
WIN_R = 8
WIN_C = 16
D_FF = -(-8 * D_MODEL // (3 * 256)) * 256
EPS = 1e-6
NEG_INF = -1e30

kernel_name = "hybrid_pool_fourier_natten_dit"


def rms_norm(x, g):
    xf = x.astype(jnp.float32)
    y = xf * lax.rsqrt(jnp.mean(xf * xf, axis=-1, keepdims=True) + EPS)
    return (y * g.astype(jnp.float32)).astype(x.dtype)


def modulate(h, shift, scale):
    return h * (1 + scale) + shift


def ada_mod(cond, w, b):
    m = jax.nn.silu(cond) @ w + b
    return jnp.split(m, 6, axis=-1)


def split_heads(t):
    return t.reshape(t.shape[0], t.shape[1], N_HEADS, HEAD_DIM)


def multiscale_pool(u, pool_w, pool_scale):
    b, s, _ = u.shape
    gw = POOL_WIDTH // POOL_GROUPS
    ug = u.reshape(b, s, POOL_GROUPS, gw).astype(jnp.float32)
    csum = jnp.cumsum(ug, axis=1)
    csum = jnp.concatenate([jnp.zeros_like(csum[:, :1]), csum], axis=1)
    t = jnp.arange(s)
    means = []
    for g, w in enumerate(POOL_WINDOWS):
        lo = jnp.clip(t - w // 2, 0, s)
        hi = jnp.clip(t + w - w // 2, 0, s)
        cnt = (hi - lo).astype(jnp.float32)
        means.append((csum[:, hi, g] - csum[:, lo, g]) / cnt[None, :, None])
    diff = (jnp.stack(means, axis=2) - ug).astype(u.dtype)
    y = jnp.einsum('bsgc,gcd->bsgd', diff, pool_w).reshape(b, s, POOL_WIDTH)
    return y * pool_scale


def fourier_mix(u, fourier_w):
    b, s, _ = u.shape
    ug = u.reshape(b, s, FOURIER_GROUPS, FOURIER_WIDTH // FOURIER_GROUPS).astype(jnp.float32)
    f = jnp.fft.fftn(ug, axes=(1, 3), norm="ortho").real.astype(u.dtype)
    return jnp.einsum('bsgc,gcd->bsgd', f, fourier_w).reshape(b, s, FOURIER_WIDTH)


def context_attention(q, k, v):
    b, n = q.shape[:2]
    s = jnp.einsum('bqhd,bkhd->bhqk', q, k).astype(jnp.float32) * (HEAD_DIM ** -0.5)
    p = jax.nn.softmax(s, axis=-1).astype(v.dtype)
    return jnp.einsum('bhqk,bkhd->bqhd', p, v).reshape(b, n, ATTN_WIDTH)


def neighbourhood_attention(q, k, v, k_ctx, v_ctx, bias_table):
    b, s = q.shape[:2]
    rows = s // GRID_W
    kr = min(WIN_R, rows)
    qg = q.reshape(b, rows, GRID_W, N_HEADS, HEAD_DIM)
    kg = k.reshape(b, rows, GRID_W, N_HEADS, HEAD_DIM)
    vg = v.reshape(b, rows, GRID_W, N_HEADS, HEAD_DIM)
    col = jnp.arange(GRID_W)
    cs = jnp.clip(col - WIN_C // 2, 0, GRID_W - WIN_C)
    col_mask = (col[None, :] >= cs[:, None]) & (col[None, :] < cs[:, None] + WIN_C)
    dc_idx = jnp.clip(col[None, :] - col[:, None] + WIN_C - 1, 0, 2 * WIN_C - 2)
    scale = HEAD_DIM ** -0.5
    n_loc = kr * GRID_W

    def row_block(r):
        rs = jnp.clip(r - kr // 2, 0, rows - kr)
        q_r = lax.dynamic_index_in_dim(qg, r, axis=1, keepdims=False)
        k_b = lax.dynamic_slice_in_dim(kg, rs, kr, axis=1)
        v_b = lax.dynamic_slice_in_dim(vg, rs, kr, axis=1)
        dr_idx = rs + jnp.arange(kr) - r + WIN_R - 1
        bias = bias_table[:, dr_idx][:, :, dc_idx].transpose(0, 2, 1, 3)
        s_loc = jnp.einsum('bchd,bijhd->bhcij', q_r, k_b).astype(jnp.float32) * scale
        s_loc = jnp.where(col_mask[None, None, :, None, :],
                          s_loc + bias.astype(jnp.float32)[None], NEG_INF)
        s_ctx = jnp.einsum('bchd,bkhd->bhck', q_r, k_ctx).astype(jnp.float32) * scale
        scores = jnp.concatenate([s_loc.reshape(b, N_HEADS, GRID_W, n_loc), s_ctx], axis=-1)
        p = jax.nn.softmax(scores, axis=-1).astype(v.dtype)
        p_loc = p[..., :n_loc].reshape(b, N_HEADS, GRID_W, kr, GRID_W)
        p_ctx = p[..., n_loc:]
        return (jnp.einsum('bhcij,bijhd->bchd', p_loc, v_b)
                + jnp.einsum('bhck,bkhd->bchd', p_ctx, v_ctx))

    o = lax.map(row_block, jnp.arange(rows))
    return jnp.moveaxis(o, 0, 1).reshape(b, s, ATTN_WIDTH)


def token_mixers(u, attn_out, pool_w, pool_scale, fourier_w, w_out):
    y = jnp.concatenate([multiscale_pool(u[..., :P_END], pool_w, pool_scale),
                         fourier_mix(u[..., P_END:F_END], fourier_w),
                         attn_out], axis=-1)
    return y @ w_out


def swiglu(h, w1, w3, w2):
    return (jax.nn.silu(h @ w1) * (h @ w3)) @ w2


def setup_inputs(seed: int = 0) -> dict:
    key = jax.random.key(seed)
    ks = jax.random.split(key, 20)
    nrm = jax.random.normal
    f32 = jnp.float32
    gw = POOL_WIDTH // POOL_GROUPS
    fw = FOURIER_WIDTH // FOURIER_GROUPS
    return {
        "x": nrm(ks[0], (BATCH, SEQ, D_MODEL), f32),
        "c": nrm(ks[1], (BATCH, D_MODEL), f32),
        "ctx": nrm(ks[2], (BATCH, CTX_LEN, D_MODEL), f32),
        "c_ctx": nrm(ks[3], (D_MODEL,), f32),
        "w_ada": nrm(ks[4], (DEPTH, D_MODEL, 6 * D_MODEL), f32) * (0.5 * D_MODEL ** -0.5),
        "b_ada": nrm(ks[5], (DEPTH, 6 * D_MODEL), f32) * 0.01,
        "norm1_g": 1.0 + 0.05 * nrm(ks[6], (DEPTH, D_MODEL), f32),
        "w_in": nrm(ks[7], (DEPTH, D_MODEL, IN_WIDTH), f32) * D_MODEL ** -0.5,
        "pool_w": nrm(ks[8], (DEPTH, POOL_GROUPS, gw, gw), f32) * gw ** -0.5,
        "pool_scale": 1.0 + 0.1 * nrm(ks[9], (DEPTH, POOL_WIDTH), f32),
        "fourier_w": nrm(ks[10], (DEPTH, FOURIER_GROUPS, fw, fw), f32) * fw ** -0.5,
        "nat_bias": 0.1 * nrm(ks[11], (DEPTH, N_HEADS, 2 * WIN_R - 1, 2 * WIN_C - 1), f32),
        "w_out": nrm(ks[12], (DEPTH, MIX_WIDTH, D_MODEL), f32) * MIX_WIDTH ** -0.5,
        "norm2_g": 1.0 + 0.05 * nrm(ks[13], (DEPTH, D_MODEL), f32),
        "w_ffn1": nrm(ks[14], (DEPTH, D_MODEL, D_FF), f32) * D_MODEL ** -0.5,
        "w_ffn3": nrm(ks[15], (DEPTH, D_MODEL, D_FF), f32) * D_MODEL ** -0.5,
        "w_ffn2": nrm(ks[16], (DEPTH, D_FF, D_MODEL), f32) * D_FF ** -0.5,
        "final_g": 1.0 + 0.05 * nrm(ks[17], (D_MODEL,), f32),
    }


def reference(x, c, ctx, c_ctx, w_ada, b_ada, norm1_g, w_in, pool_w, pool_scale, fourier_w,
              nat_bias, w_out, norm2_g, w_ffn1, w_ffn3, w_ffn2, final_g):
    for l in range(DEPTH):
        last = l == DEPTH - 1
        sh1, sc1, g1, sh2, sc2, g2 = [m[:, None, :] for m in ada_mod(c, w_ada[l], b_ada[l])]
        csh1, csc1, cg1, csh2, csc2, cg2 = ada_mod(c_ctx, w_ada[l], b_ada[l])

        hc = modulate(rms_norm(ctx, norm1_g[l]), csh1, csc1)
        if last:
            kvc = hc @ w_in[l][:, Q_END:]
            kc, vc = split_heads(kvc[..., :ATTN_WIDTH]), split_heads(kvc[..., ATTN_WIDTH:])
        else:
            uc = hc @ w_in[l]
            kc, vc = split_heads(uc[..., Q_END:K_END]), split_heads(uc[..., K_END:])

        hx = modulate(rms_norm(x, norm1_g[l]), sh1, sc1)
        ux = hx @ w_in[l]
        attn_x = neighbourhood_attention(split_heads(ux[..., F_END:Q_END]),
                                         split_heads(ux[..., Q_END:K_END]),
                                         split_heads(ux[..., K_END:]), kc, vc, nat_bias[l])
        x = x + g1 * token_mixers(ux, attn_x, pool_w[l], pool_scale[l], fourier_w[l], w_out[l])
        x = x + g2 * swiglu(modulate(rms_norm(x, norm2_g[l]), sh2, sc2),
                            w_ffn1[l], w_ffn3[l], w_ffn2[l])

        if not last:
            attn_c = context_attention(split_heads(uc[..., F_END:Q_END]), kc, vc)
            ctx = ctx + cg1 * token_mixers(uc, attn_c, pool_w[l], pool_scale[l], fourier_w[l], w_out[l])
            ctx = ctx + cg2 * swiglu(modulate(rms_norm(ctx, norm2_g[l]), csh2, csc2),
                                     w_ffn1[l], w_ffn3[l], w_ffn2[l])
    return rms_norm(x, final_g)
```

```python
import bisect
import os
from contextlib import ExitStack

import numpy as np
import ml_dtypes

import concourse.bass as bass
import concourse.mybir as mybir
from concourse.bass_utils import run_bass_kernel_spmd

F32 = mybir.dt.float32
BF16 = mybir.dt.bfloat16
AF = mybir.ActivationFunctionType
ALU = mybir.AluOpType
AX = mybir.AxisListType

D = 1024
L = 4
S = 8192
CT = 256
NTOK = S + CT
NT = NTOK // 128
DFF = 2816
NF = DFF // 128
EPS = 1e-6
WCOLS = 2304
NEG = -1e30
NCORES = 4


class Res:
    __slots__ = ("name", "w", "rs")

    def __init__(self, name=""):
        self.name = name
        self.w = None
        self.rs = []


class Op:
    __slots__ = ("eng", "fn", "dma", "gidx", "deps", "signal", "cnt", "seq")


ENGS = ["pe", "act", "dve", "pool", "sp"]


class Tracker:
    def __init__(self, nc, stack):
        self.nc = nc
        self.stack = stack
        self.esem = {e: stack.enter_context(nc.semaphore("s_" + e)) for e in ENGS}
        self.ecount = {e: 0 for e in ENGS}
        self.dsem = {}
        self.dcount = {}
        self.ops = []
        self.allres = []
        self.total = 0

    def R(self, name=""):
        r = Res(name)
        self.allres.append(r)
        return r

    def op(self, eng, fn, r=(), w=(), dma=None):
        o = Op()
        o.eng = eng
        o.fn = fn
        o.dma = dma
        o.gidx = len(self.ops)
        o.signal = False
        deps = set()
        for b in r:
            if b.w is not None:
                deps.add(b.w)
        for b in w:
            if b.w is not None:
                deps.add(b.w)
            deps.update(b.rs)
        for b in r:
            b.rs.append(o)
        for b in w:
            b.w = o
            b.rs = []
        deps.discard(o)
        o.deps = deps
        if dma is not None and dma not in self.dsem:
            self.dsem[dma] = self.stack.enter_context(self.nc.semaphore("d_" + str(dma)))
            self.dcount[dma] = 0
        self.ops.append(o)
        return o

    def end_phase(self):
        nc = self.nc
        ops = self.ops
        dma_ops = [o for o in ops if o.dma is not None]
        fin = Op()
        fin.eng = "sp"; fin.fn = None; fin.dma = None; fin.gidx = len(ops); fin.signal = False
        fin.deps = set(dma_ops)
        ops.append(fin)
        per = {e: [] for e in ENGS}
        for o in ops:
            o.seq = len(per[o.eng])
            per[o.eng].append(o)
        dkeys = {}
        for o in dma_ops:
            dkeys.setdefault(o.dma, []).append(o.gidx)
        plan = {}
        for o in ops:
            cdeps = {}
            ddeps = {}
            for d in o.deps:
                if d.dma is not None:
                    n = bisect.bisect_left(dkeys[d.dma], o.gidx)
                    ddeps[d.dma] = 16 * (self.dcount[d.dma] + n)
                else:
                    if d.eng == "pe" and o.eng == "pe":
                        continue
                    if d.eng not in cdeps or cdeps[d.eng].seq < d.seq:
                        cdeps[d.eng] = d
            for d in cdeps.values():
                d.signal = True
            plan[o.gidx] = (cdeps, ddeps)
        for e in ENGS:
            c = self.ecount[e]
            for o in per[e]:
                if o.signal:
                    c += 1
                o.cnt = c
            self.ecount[e] = c
        for k, lst in dkeys.items():
            self.dcount[k] += len(lst)
        esem = self.esem
        dsem = self.dsem

        def run(eng_name, eh):
            known = {}
            self.icount = getattr(self, "icount", {})
            for o in per[eng_name]:
                self.icount[eng_name] = self.icount.get(eng_name, 0) + 1 + sum(1 for d in plan[o.gidx][0].values() if known.get(("e", d.eng), -1) < d.cnt) + sum(1 for k, v in plan[o.gidx][1].items() if known.get(("d", k), -1) < v)
                cdeps, ddeps = plan[o.gidx]
                for d in cdeps.values():
                    if known.get(("e", d.eng), -1) < d.cnt:
                        eh.wait_ge(esem[d.eng], d.cnt)
                        known[("e", d.eng)] = d.cnt
                for k, v in ddeps.items():
                    if known.get(("d", k), -1) < v:
                        eh.wait_ge(dsem[k], v)
                        known[("d", k)] = v
                if o.fn is None:
                    continue
                ins = o.fn(eh)
                if o.dma is not None:
                    ins.then_inc(dsem[o.dma], 16)
                elif o.signal:
                    ins.then_inc(esem[eng_name], 1)

        with nc.Block() as block:
            @block.tensor
            def _(t):
                run("pe", t)

            @block.scalar
            def _(t):
                run("act", t)

            @block.vector
            def _(t):
                run("dve", t)

            @block.gpsimd
            def _(t):
                run("pool", t)

            @block.sync
            def _(t):
                run("sp", t)
        nc.all_engine_barrier()
        self.total += len(ops)
        self.ops = []
        for r_ in self.allres:
            r_.w = None
            r_.rs = []
        self.allres = []


_UC = [0]


def U(name):
    _UC[0] += 1
    return "%s_%d" % (name, _UC[0])


def bcast_rows(ap2d, nparts=128):
    n = ap2d.shape[-1]
    return bass.AP(tensor=ap2d.tensor, offset=ap2d.offset, ap=[[0, nparts], [1, n]])


def super_tiles():
    st = [(4 * i, 4, 0) for i in range(16)]
    st.append((64, 2, 1))
    return st


def build(nlayers=L, dbg=False, force_ctx=False, ctx_mask=None, only=None):
    nc = bass.Bass("TRN2", target_bir_lowering=False)

    def din(name, shape, dt=F32):
        return nc.dram_tensor(name, list(shape), dt, kind="ExternalInput").ap()

    def dscr(name, shape, dt):
        return nc.dram_tensor(name, list(shape), dt, kind=("ExternalOutput" if dbg else "Internal")).ap()

    x_in = din("x", [S, D])
    ctx_in = din("ctx", [CT, D])
    cc = din("cc", [128, 16])
    w_ada = din("w_ada", [L, D, 6 * D])
    b_ada = din("b_ada", [L, 6 * D])
    n1g = din("norm1_g", [L, D])
    n2g = din("norm2_g", [L, D])
    w_in = din("w_in", [L, D, 2048])
    pool_w = din("pool_w", [L, 4, 64, 64])
    pscol = din("pscol", [L, 64, 4])
    fwbd = din("fwbd", [L, 2, 128, 128])
    w_out = din("w_out", [L, D, D])
    w1 = din("w_ffn1", [L, D, DFF])
    w3 = din("w_ffn3", [L, D, DFF])
    w2 = din("w_ffn2", [L, DFF, D])
    fing = din("final_g", [1, D])
    identb_d = din("identb", [128, 128], BF16)
    ccbd_d = din("ccbd", [2, 128, 128], BF16)
    pb_d = din("pb", [4, 3, 3, 128, 128], BF16)
    f1_d = din("f1", [128, 128], BF16)
    m2_d = din("m2", [128, 64 * 2 * 128], BF16)
    cd_d = din("cd", [128, 2 * 2 * 256], BF16)
    bm_d = din("bm", [L, 5, 128, 8 * 640], BF16)
    out = nc.dram_tensor("out", [S, D], F32, kind="ExternalOutput").ap()

    XS = dscr("XS", [NTOK, D], F32)
    MOD = dscr("MOD", [L, 2, 128, 6 * D], F32)
    QT = dscr("QT", [512, NTOK], BF16)
    KT = dscr("KT", [512, NTOK], BF16)
    VV = dscr("VV", [NTOK, 512], BF16)
    UP = dscr("UP", [NTOK, 256], BF16)
    ZC = dscr("ZC", [NTOK, 256], BF16)
    ZS = dscr("ZS", [NTOK, 256], BF16)
    ABd = dscr("ABd", [128, 128 * 256], BF16)
    MIXT = dscr("MIXT", [D + 128, NTOK], BF16)

    with ExitStack() as stack:
        T = Tracker(nc, stack)

        def phase_prologue():
            with ExitStack() as es:
                wada = es.enter_context(nc.sbuf_tensor(U("wada"), [128, 8, 6 * D], BF16))
                bias = es.enter_context(nc.sbuf_tensor(U("bias"), [128, 6 * D], F32))
                m6 = es.enter_context(nc.sbuf_tensor(U("m6"), [128, 6 * D], F32))
                gn = es.enter_context(nc.sbuf_tensor(U("gn"), [128, 2, D], F32))
                ccs = es.enter_context(nc.sbuf_tensor(U("ccs"), [128, 16], F32))
                sil = es.enter_context(nc.sbuf_tensor(U("sil"), [128, 16], F32))
                ones = es.enter_context(nc.sbuf_tensor(U("ones"), [128, 128], F32))
                rep = es.enter_context(nc.sbuf_tensor(U("rep"), [128, 16, 128], BF16))
                xcp = es.enter_context(nc.sbuf_tensor(U("xcp"), [128, 2, 4 * D], F32))
                pp = es.enter_context(nc.psum_tensor(U("pp"), [128, 2, 512], F32))
                r_wada = T.R(); r_bias = T.R(); r_m6 = T.R(); r_gn = T.R(); r_cc = T.R(); r_sil = T.R()
                r_ones = T.R(); r_rep = T.R(); r_pp = [T.R(), T.R()]; r_xcp = [T.R(), T.R()]
                for i in range(17):
                    t0, ntl = (4 * i, 4) if i < 16 else (64, 2)
                    sl = i % 2
                    src = (x_in[t0 * 128:(t0 + ntl) * 128, :] if i < 16 else ctx_in[:, :])
                    T.op("sp", lambda e, sl=sl, src=src, ntl=ntl: e.dma_start(
                        out=xcp[:, sl, 0:ntl * D].rearrange("p (t d) -> p t d", t=ntl),
                        in_=src.rearrange("(t p) d -> p t d", p=128)), w=[r_xcp[sl]], dma="xcp%d" % sl)
                    T.op("sp", lambda e, sl=sl, t0=t0, ntl=ntl: e.dma_start(
                        out=XS[t0 * 128:(t0 + ntl) * 128, :].rearrange("(t p) d -> p t d", p=128),
                        in_=xcp[:, sl, 0:ntl * D].rearrange("p (t d) -> p t d", t=ntl)), r=[r_xcp[sl]], dma="xcp%d" % sl)
                T.op("sp", lambda e: e.dma_start(out=ccs[:], in_=cc), w=[r_cc], dma="cc")
                T.op("pool", lambda e: e.memset(ones[:], 1.0), w=[r_ones])
                T.op("act", lambda e: e.activation(out=sil[:], in_=ccs[:], func=AF.Silu), r=[r_cc], w=[r_sil])
                for i in range(16):
                    T.op("dve", lambda e, i=i: e.tensor_scalar(out=rep[:, i, :], in0=ones[:], scalar1=sil[:, i:i + 1],
                                                                scalar2=None, op0=ALU.mult), r=[r_ones, r_sil], w=[r_rep])
                for l in range(nlayers):
                    for ck in range(8):
                        T.op("pool", lambda e, l=l, ck=ck: e.dma_start(out=wada[:, ck, :], in_=w_ada[l, ck * 128:(ck + 1) * 128, :]),
                             w=[r_wada], dma="wada")
                    T.op("sp", lambda e, l=l: e.dma_start(out=bias[:], in_=bcast_rows(b_ada[l:l + 1, :])), w=[r_bias], dma="bias")
                    T.op("sp", lambda e, l=l: e.dma_start(out=gn[:, 0, :], in_=bcast_rows(n1g[l:l + 1, :])), w=[r_gn], dma="gn")
                    T.op("sp", lambda e, l=l: e.dma_start(out=gn[:, 1, :], in_=bcast_rows(n2g[l:l + 1, :])), w=[r_gn], dma="gn")
                    for k in range(2):
                        for n in range(12):
                            pb = n % 2
                            for ck in range(8):
                                T.op("pe", lambda e, pb=pb, k=k, ck=ck, n=n: e.matmul(
                                    pp[:, pb, :], rep[:, k * 8 + ck, :], wada[:, ck, n * 512:(n + 1) * 512],
                                    start=(ck == 0), stop=(ck == 7)), r=[r_rep, r_wada], w=[r_pp[pb]])
                            T.op("dve", lambda e, pb=pb, n=n: e.tensor_tensor(
                                out=m6[:, n * 512:(n + 1) * 512], in0=pp[:, pb, :], in1=bias[:, n * 512:(n + 1) * 512], op=ALU.add),
                                r=[r_pp[pb], r_bias], w=[r_m6])
                        for j, slot in enumerate((1, 4)):
                            T.op("dve", lambda e, j=j, slot=slot: e.scalar_tensor_tensor(
                                out=m6[:, slot * D:(slot + 1) * D], in0=m6[:, slot * D:(slot + 1) * D], scalar=1.0,
                                in1=gn[:, j, :], op0=ALU.add, op1=ALU.mult), r=[r_gn, r_m6], w=[r_m6])
                        T.op("sp", lambda e, l=l, k=k: e.dma_start(out=MOD[l, k], in_=m6[:]), r=[r_m6], dma="m6")
                T.end_phase()

        def phase_A(l):
            with ExitStack() as es:
                weff = es.enter_context(nc.sbuf_tensor(U("weff"), [128, 8, WCOLS], BF16))
                wf = es.enter_context(nc.sbuf_tensor(U("wf"), [128, 8, 256], BF16))
                wft = es.enter_context(nc.sbuf_tensor(U("wft"), [128, 2, D], BF16))
                bd = es.enter_context(nc.sbuf_tensor(U("bd"), [128, 2, 512], BF16))
                fwb = es.enter_context(nc.sbuf_tensor(U("fwb"), [128, 2, 128], BF16))
                ccb = es.enter_context(nc.sbuf_tensor(U("ccb"), [128, 2, 128], BF16))
                idb = es.enter_context(nc.sbuf_tensor(U("idb"), [128, 128], BF16))
                modt = es.enter_context(nc.sbuf_tensor(U("modt"), [128, 2, 2, D], F32))
                xt = es.enter_context(nc.sbuf_tensor(U("xt"), [128, 2, 4, D], F32))
                junk = es.enter_context(nc.sbuf_tensor(U("junk"), [128, D], F32))
                ss = es.enter_context(nc.sbuf_tensor(U("ss"), [128, 8], F32))
                hb = es.enter_context(nc.sbuf_tensor(U("hb"), [128, 2, D], BF16))
                hT = es.enter_context(nc.sbuf_tensor(U("hT"), [128, 2, 8, 512], BF16))
                qk = es.enter_context(nc.sbuf_tensor(U("qk"), [128, 2, 8, 512], BF16))
                tm = es.enter_context(nc.sbuf_tensor(U("tm"), [128, 2, 4, 1280], BF16))
                ptr = es.enter_context(nc.psum_tensor(U("ptr"), [128, 2, 8, 128], BF16))
                pq = es.enter_context(nc.psum_tensor(U("pq"), [128, 2, 512], F32))
                pt_ = es.enter_context(nc.psum_tensor(U("pt"), [128, 2, 512], F32))
                r_weff = T.R(); r_wf = T.R(); r_wft = T.R(); r_bd = T.R(); r_fwb = T.R(); r_ccb = T.R(); r_idb = T.R()
                r_mod = T.R(); r_xt = [T.R(), T.R()]; r_junk = T.R(); r_ss = [T.R(), T.R()]; r_hb = [T.R(), T.R()]
                r_hT = [T.R(), T.R()]; r_qk = [T.R(), T.R()]; r_tm = [T.R(), T.R()]
                r_ptr = [T.R(), T.R()]; r_pq = [T.R(), T.R()]; r_pt = [T.R(), T.R()]
                T.op("sp", lambda e: e.dma_start(out=idb[:], in_=identb_d), w=[r_idb], dma="idb")
                T.op("sp", lambda e: e.dma_start(out=ccb[:], in_=ccbd_d.rearrange("a p q -> p a q")), w=[r_ccb], dma="ccb")
                T.op("pool", lambda e: e.dma_start(out=fwb[:], in_=fwbd[l].rearrange("a p q -> p a q")), w=[r_fwb], dma="fwb")
                for ck in range(8):
                    rows = w_in[l, ck * 128:(ck + 1) * 128, :]
                    T.op("pool", lambda e, ck=ck, rows=rows: e.dma_start(out=weff[:, ck, 0:256], in_=rows[:, 0:256]), w=[r_weff], dma="weff")
                    T.op("pool", lambda e, ck=ck, rows=rows: e.dma_start(out=weff[:, ck, 768:WCOLS], in_=rows[:, 512:2048]), w=[r_weff], dma="weff")
                    T.op("pool", lambda e, ck=ck, rows=rows: e.dma_start(out=wf[:, ck, :], in_=rows[:, 256:512]), w=[r_wf], dma="wf")
                for k in range(2):
                    T.op("sp", lambda e, k=k: e.dma_start(out=modt[:, k, :, :], in_=MOD[l, k][:, 0:2 * D].rearrange("p (a d) -> p a d", a=2)),
                         w=[r_mod], dma="modt")
                T.op("pool", lambda e: e.memset(bd[:], 0.0), w=[r_bd])
                for c2 in range(2):
                    for cs_ in range(2):
                        pb = (c2 * 2 + cs_) % 2
                        T.op("pe", lambda e, c2=c2, cs_=cs_, pb=pb: e.matmul(pq[:, pb, 0:128], ccb[:, cs_, :], fwb[:, c2, :], start=True, stop=True),
                             r=[r_ccb, r_fwb], w=[r_pq[pb]])
                        T.op("dve", lambda e, c2=c2, cs_=cs_, pb=pb: e.tensor_copy(bd[:, c2, cs_ * 256 + c2 * 128: cs_ * 256 + (c2 + 1) * 128], pq[:, pb, 0:128]),
                             r=[r_pq[pb]], w=[r_bd])
                for ck in range(8):
                    pb = ck % 2
                    for c2 in range(2):
                        T.op("pe", lambda e, ck=ck, c2=c2, pb=pb: e.transpose(ptr[:, pb, c2, :], wf[:, ck, c2 * 128:(c2 + 1) * 128], idb[:]),
                             r=[r_wf, r_idb], w=[r_ptr[pb]])
                    T.op("dve", lambda e, ck=ck, pb=pb: e.tensor_copy(wft[:, :, ck * 128:(ck + 1) * 128], ptr[:, pb, 0:2, :]),
                         r=[r_ptr[pb]], w=[r_wft])
                for ck in range(8):
                    pb = ck % 2
                    for c2 in range(2):
                        T.op("pe", lambda e, ck=ck, c2=c2, pb=pb: e.matmul(pq[:, pb, :], wft[:, c2, ck * 128:(ck + 1) * 128], bd[:, c2, :],
                                                                        start=(c2 == 0), stop=(c2 == 1)), r=[r_wft, r_bd], w=[r_pq[pb]])
                    T.op("dve", lambda e, ck=ck, pb=pb: e.tensor_copy(weff[:, ck, 256:768], pq[:, pb, :]), r=[r_pq[pb]], w=[r_weff])
                stsA = super_tiles()

                def stageA1(si):
                    t0, ntl, k = stsA[si]
                    sl = si % 2
                    ntok = ntl * 128
                    T.op("sp", lambda e, sl=sl, t0=t0, ntl=ntl: e.dma_start(
                        out=xt[:, sl, 0:ntl, :], in_=XS[t0 * 128:(t0 + ntl) * 128, :].rearrange("(t p) d -> p t d", p=128)),
                        w=[r_xt[sl]], dma="xt%d" % sl)
                    for t in range(ntl):
                        T.op("act", lambda e, sl=sl, t=t: e.activation(out=junk[:], in_=xt[:, sl, t, :], func=AF.Square,
                                                                       accum_out=ss[:, sl * 4 + t: sl * 4 + t + 1]),
                             r=[r_xt[sl]], w=[r_junk, r_ss[sl]])
                    T.op("dve", lambda e, sl=sl: e.tensor_scalar(out=ss[:, sl * 4:sl * 4 + 4], in0=ss[:, sl * 4:sl * 4 + 4], scalar1=1.0 / D,
                                                                 scalar2=EPS, op0=ALU.mult, op1=ALU.add), r=[r_ss[sl]], w=[r_ss[sl]])
                    T.op("act", lambda e, sl=sl: e.activation(out=ss[:, sl * 4:sl * 4 + 4], in_=ss[:, sl * 4:sl * 4 + 4], func=AF.Sqrt), r=[r_ss[sl]], w=[r_ss[sl]])
                    T.op("dve", lambda e, sl=sl: e.reciprocal(out=ss[:, sl * 4:sl * 4 + 4], in_=ss[:, sl * 4:sl * 4 + 4]), r=[r_ss[sl]], w=[r_ss[sl]])
                    for t in range(ntl):
                        hs = t % 2
                        T.op("dve", lambda e, sl=sl, t=t, k=k: e.scalar_tensor_tensor(
                            out=xt[:, sl, t, :], in0=xt[:, sl, t, :], scalar=ss[:, sl * 4 + t: sl * 4 + t + 1], in1=modt[:, k, 1, :],
                            op0=ALU.mult, op1=ALU.mult), r=[r_ss[sl], r_mod, r_xt[sl]], w=[r_xt[sl]])
                        T.op("pool", lambda e, sl=sl, t=t, k=k, hs=hs: e.tensor_tensor(out=hb[:, hs, :], in0=xt[:, sl, t, :], in1=modt[:, k, 0, :], op=ALU.add),
                             r=[r_xt[sl], r_mod], w=[r_hb[hs]])
                        for ck in range(8):
                            T.op("pe", lambda e, hs=hs, ck=ck: e.transpose(ptr[:, hs, ck, :], hb[:, hs, ck * 128:(ck + 1) * 128], idb[:]),
                                 r=[r_hb[hs], r_idb], w=[r_ptr[hs]])
                        T.op("act", lambda e, sl=sl, t=t, hs=hs: e.copy(out=hT[:, sl, :, t * 128:(t + 1) * 128], in_=ptr[:, hs, :, :]),
                             r=[r_ptr[hs]], w=[r_hT[sl]])
                def stageA2(si):
                    t0, ntl, k = stsA[si]
                    sl = si % 2
                    ntok = ntl * 128
                    for c in range(8):
                        pb = c % 2
                        col0 = 768 + c * 128
                        for ck in range(8):
                            T.op("pe", lambda e, pb=pb, ck=ck, col0=col0, sl=sl, ntok=ntok: e.matmul(
                                pq[:, pb, 0:ntok], weff[:, ck, col0:col0 + 128], hT[:, sl, ck, 0:ntok], start=(ck == 0), stop=(ck == 7)),
                                r=[r_weff, r_hT[sl]], w=[r_pq[pb]])
                        T.op("act", lambda e, pb=pb, c=c, sl=sl, ntok=ntok: e.activation(
                            out=qk[:, sl, c, 0:ntok], in_=pq[:, pb, 0:ntok], func=AF.Identity, scale=(0.125 if c < 4 else 1.0)),
                            r=[r_pq[pb]], w=[r_qk[sl]])
                    T.op("sp", lambda e, sl=sl, t0=t0, ntok=ntok: e.dma_start(
                        out=QT[:, t0 * 128:t0 * 128 + ntok].rearrange("(c p) t -> p c t", p=128), in_=qk[:, sl, 0:4, 0:ntok]),
                        r=[r_qk[sl]], dma="qk%d" % sl)
                    T.op("sp", lambda e, sl=sl, t0=t0, ntok=ntok: e.dma_start(
                        out=KT[:, t0 * 128:t0 * 128 + ntok].rearrange("(c p) t -> p c t", p=128), in_=qk[:, sl, 4:8, 0:ntok]),
                        r=[r_qk[sl]], dma="qk%d" % sl)
                    for t in range(ntl):
                        for gi, (c0, c1, o0) in enumerate(((0, 512, 0), (512, 768, 512), (1792, 2304, 768))):
                            pb = (t * 3 + gi) % 2
                            n = c1 - c0
                            for ck in range(8):
                                T.op("pe", lambda e, pb=pb, ck=ck, c0=c0, c1=c1, n=n, sl=sl, t=t: e.matmul(
                                    pt_[:, pb, 0:n], hT[:, sl, ck, t * 128:(t + 1) * 128], weff[:, ck, c0:c1], start=(ck == 0), stop=(ck == 7)),
                                    r=[r_weff, r_hT[sl]], w=[r_pt[pb]])
                            T.op("dve", lambda e, pb=pb, n=n, o0=o0, sl=sl, t=t: e.tensor_copy(tm[:, sl, t, o0:o0 + n], pt_[:, pb, 0:n]),
                                 r=[r_pt[pb]], w=[r_tm[sl]])
                    for (dst, o0, n) in ((UP, 0, 256), (ZC, 256, 256), (ZS, 512, 256), (VV, 768, 512)):
                        T.op("sp", lambda e, dst=dst, o0=o0, n=n, sl=sl, t0=t0, ntl=ntl: e.dma_start(
                            out=dst[t0 * 128:(t0 + ntl) * 128, :].rearrange("(t p) c -> p t c", p=128), in_=tm[:, sl, 0:ntl, o0:o0 + n]),
                            r=[r_tm[sl]], dma="tm%d" % sl)

                stageA1(0)
                for si in range(len(stsA)):
                    if si + 1 < len(stsA):
                        stageA1(si + 1)
                    stageA2(si)
                T.end_phase()

        def phase_pool(l, do_ctx):
            with ExitStack() as es:
                pbm = es.enter_context(nc.sbuf_tensor(U("pbm"), [128, 36, 128], BF16))
                pw = es.enter_context(nc.sbuf_tensor(U("pw"), [64, 4, 64], BF16))
                psc = es.enter_context(nc.sbuf_tensor(U("psc"), [64, 4], F32))
                ut = es.enter_context(nc.sbuf_tensor(U("ut"), [128, 2, 6, 256], BF16))
                dT = es.enter_context(nc.sbuf_tensor(U("dT"), [64, 2, 4, 512], BF16))
                yT = es.enter_context(nc.sbuf_tensor(U("yT"), [64, 2, 4, 512], BF16))
                pd = es.enter_context(nc.psum_tensor(U("pd"), [64, 2, 512], F32))
                py = es.enter_context(nc.psum_tensor(U("py"), [64, 2, 512], F32))
                r_pbm = T.R(); r_pw = T.R(); r_psc = T.R(); r_ut = [T.R(), T.R()]; r_dT = [T.R(), T.R()]; r_yT = [T.R(), T.R()]
                r_pd = [T.R(), T.R()]; r_py = [T.R(), T.R()]
                T.op("sp", lambda e: e.dma_start(out=pbm[:], in_=pb_d.rearrange("g k o p q -> p (g k o) q")), w=[r_pbm], dma="pbm")
                T.op("pool", lambda e: e.dma_start(out=pw[:], in_=pool_w[l].rearrange("g c d -> c g d")), w=[r_pw], dma="pw")
                T.op("sp", lambda e: e.dma_start(out=psc[:], in_=pscol[l]), w=[r_psc], dma="psc")
                sts = super_tiles() if do_ctx else super_tiles()[:16]
                for si, (t0, ntl, k) in enumerate(sts):
                    sl = si % 2
                    seq0, seqn = (0, 64) if k == 0 else (64, 2)
                    lo = max(t0 - 1, seq0); hi = min(t0 + ntl + 1, seq0 + seqn)
                    off = lo - (t0 - 1)
                    T.op("sp", lambda e, sl=sl, lo=lo, hi=hi, off=off: e.dma_start(
                        out=ut[:, sl, off:off + hi - lo, :], in_=UP[lo * 128:hi * 128, :].rearrange("(t p) c -> p t c", p=128)),
                        w=[r_ut[sl]], dma="ut%d" % sl)
                    for g in range(4):
                        pb = g % 2
                        for t in range(ntl):
                            tile = t0 + t
                            kind = 0 if tile == seq0 else (2 if tile == seq0 + seqn - 1 else 1)
                            offs = [o for o in (-1, 0, 1) if not (kind == 0 and o == -1) and not (kind == 2 and o == 1)]
                            for oi, o in enumerate(offs):
                                T.op("pe", lambda e, pb=pb, g=g, t=t, o=o, kind=kind, sl=sl, first=(oi == 0), last=(oi == len(offs) - 1): e.matmul(
                                    pd[:, pb, t * 128:(t + 1) * 128], ut[:, sl, t + 1 + o, g * 64:(g + 1) * 64], pbm[:, (g * 3 + kind) * 3 + o + 1, :],
                                    start=first, stop=last), r=[r_ut[sl], r_pbm], w=[r_pd[pb]])
                        T.op("dve", lambda e, pb=pb, g=g, sl=sl, ntl=ntl: e.tensor_copy(dT[:, sl, g, 0:ntl * 128], pd[:, pb, 0:ntl * 128]),
                             r=[r_pd[pb]], w=[r_dT[sl]])
                        T.op("pe", lambda e, pb=pb, g=g, sl=sl, ntl=ntl: e.matmul(py[:, pb, 0:ntl * 128], pw[:, g, :], dT[:, sl, g, 0:ntl * 128], start=True, stop=True),
                             r=[r_pw, r_dT[sl]], w=[r_py[pb]])
                        T.op("act", lambda e, pb=pb, g=g, sl=sl, ntl=ntl: e.activation(out=yT[:, sl, g, 0:ntl * 128], in_=py[:, pb, 0:ntl * 128], func=AF.Identity,
                                                                                   scale=psc[:, g:g + 1]), r=[r_py[pb], r_psc], w=[r_yT[sl]])
                    if k == 1 and os.environ.get("POOLCTX_ALT") == "dst0":
                        T.op("sp", lambda e, sl=sl, t0=t0, ntl=ntl: e.dma_start(
                            out=MIXT[0:256, 0:ntl * 128].rearrange("(g d) t -> d g t", d=64), in_=yT[:, sl, :, 0:ntl * 128]),
                            r=[r_yT[sl]], dma="yT%d" % sl)
                    elif k == 1:
                        for g in range(4):
                            T.op("sp", lambda e, sl=sl, t0=t0, ntl=ntl, g=g: e.dma_start(
                                out=MIXT[g * 64:(g + 1) * 64, t0 * 128:(t0 + ntl) * 128], in_=yT[:, sl, g, 0:ntl * 128]),
                                r=[r_yT[sl]], dma="yT%d" % sl)
                    else:
                        T.op("sp", lambda e, sl=sl, t0=t0, ntl=ntl: e.dma_start(
                            out=MIXT[0:256, t0 * 128:(t0 + ntl) * 128].rearrange("(g d) t -> d g t", d=64), in_=yT[:, sl, :, 0:ntl * 128]),
                            r=[r_yT[sl]], dma="yT%d" % sl)
                T.end_phase()

        def phase_f1():
            with ExitStack() as es:
                z = es.enter_context(nc.sbuf_tensor(U("z"), [128, 128 * 256], BF16))
                ab = es.enter_context(nc.sbuf_tensor(U("ab"), [128, 128 * 256], BF16))
                f1 = es.enter_context(nc.sbuf_tensor(U("f1"), [128, 128], BF16))
                pf = es.enter_context(nc.psum_tensor(U("pf"), [128, 4, 512], F32))
                r_z = T.R(); r_ab = T.R(); r_f1 = T.R(); r_pf = [T.R() for _ in range(4)]
                T.op("sp", lambda e: e.dma_start(out=f1[:], in_=f1_d), w=[r_f1], dma="f1")
                for h_, src in enumerate((ZC, ZS)):
                    for q in range(4):
                        T.op("sp", lambda e, h_=h_, src=src, q=q: e.dma_start(
                            out=z[h_ * 64 + q * 16: h_ * 64 + (q + 1) * 16, :],
                            in_=src[q * 2048:(q + 1) * 2048, :].rearrange("(t s) c -> t (s c)", s=128)), w=[r_z], dma="z")
                for n in range(64):
                    pb = n % 4
                    T.op("pe", lambda e, n=n, pb=pb: e.matmul(pf[:, pb, :], f1[:], z[:, n * 512:(n + 1) * 512], start=True, stop=True),
                         r=[r_f1, r_z], w=[r_pf[pb]])
                    eng = "dve" if n % 2 == 0 else "act"
                    if eng == "dve":
                        T.op("dve", lambda e, n=n, pb=pb: e.tensor_copy(ab[:, n * 512:(n + 1) * 512], pf[:, pb, :]), r=[r_pf[pb]], w=[r_ab])
                    else:
                        T.op("act", lambda e, n=n, pb=pb: e.copy(out=ab[:, n * 512:(n + 1) * 512], in_=pf[:, pb, :]), r=[r_pf[pb]], w=[r_ab])
                for q in range(4):
                    T.op("sp", lambda e, q=q: e.dma_start(out=ABd[:, q * 8192:(q + 1) * 8192], in_=ab[:, q * 8192:(q + 1) * 8192]), r=[r_ab], dma="ab")
                T.end_phase()

        def phase_f2(do_ctx):
            with ExitStack() as es:
                abt = es.enter_context(nc.sbuf_tensor(U("abt"), [128, 128, 256], BF16))
                m2 = es.enter_context(nc.sbuf_tensor(U("m2"), [128, 64, 2, 128], BF16))
                yf = es.enter_context(nc.sbuf_tensor(U("yf"), [128, 2, S], BF16))
                zc = es.enter_context(nc.sbuf_tensor(U("zc"), [128, 2, 2, 256], BF16))
                cd = es.enter_context(nc.sbuf_tensor(U("cd"), [128, 2, 2, 256], BF16))
                yfc = es.enter_context(nc.sbuf_tensor(U("yfc"), [128, 2, 256], BF16))
                pg = es.enter_context(nc.psum_tensor(U("pg"), [128, 2, 4, 128], F32))
                pc = es.enter_context(nc.psum_tensor(U("pc"), [128, 2, 512], F32))
                r_abt = T.R(); r_m2 = T.R(); r_yf = T.R(); r_zc = T.R(); r_cd = T.R(); r_yfc = T.R()
                r_pg = [T.R(), T.R()]; r_pc = [T.R(), T.R()]
                T.op("sp", lambda e: e.dma_start(out=m2[:], in_=m2_d.rearrange("p (k a j) -> p k a j", k=64, a=2)), w=[r_m2], dma="m2")
                abv = ABd.rearrange("q (s c) -> s q c", c=256)
                for q in range(8):
                    T.op("sp", lambda e, q=q: e.dma_start(out=abt[:, q * 16:(q + 1) * 16, :], in_=abv[:, q * 16:(q + 1) * 16, :]), w=[r_abt], dma="abt")
                yv = yf[:].rearrange("p c (k1 k2) -> p c k2 k1", k2=64)
                it = 0
                for c2 in range(2):
                    for kq in range(16):
                        pb = it % 2
                        it += 1
                        for ki in range(4):
                            k2 = kq * 4 + ki
                            T.op("pe", lambda e, pb=pb, ki=ki, k2=k2, c2=c2: e.matmul(pg[:, pb, ki, :], abt[:, k2, c2 * 128:(c2 + 1) * 128], m2[:, k2, 0, :],
                                                                                start=True, stop=False), r=[r_abt, r_m2], w=[r_pg[pb]])
                            T.op("pe", lambda e, pb=pb, ki=ki, k2=k2, c2=c2: e.matmul(pg[:, pb, ki, :], abt[:, 64 + k2, c2 * 128:(c2 + 1) * 128], m2[:, k2, 1, :],
                                                                                start=False, stop=True), r=[r_abt, r_m2], w=[r_pg[pb]])
                        if it % 2 == 0:
                            T.op("dve", lambda e, pb=pb, kq=kq, c2=c2: e.tensor_copy(yv[:, c2, kq * 4:(kq + 1) * 4, :], pg[:, pb, :, :]), r=[r_pg[pb]], w=[r_yf])
                        else:
                            T.op("act", lambda e, pb=pb, kq=kq, c2=c2: e.copy(out=yv[:, c2, kq * 4:(kq + 1) * 4, :], in_=pg[:, pb, :, :]), r=[r_pg[pb]], w=[r_yf])
                for c2 in range(2):
                    T.op("sp", lambda e, c2=c2: e.dma_start(out=MIXT[256 + c2 * 128: 256 + (c2 + 1) * 128, 0:S], in_=yf[:, c2, :]), r=[r_yf], dma="yf")
                if do_ctx:
                    T.op("sp", lambda e: e.dma_start(out=cd[:], in_=cd_d.rearrange("p (t a k) -> p t a k", t=2, a=2)), w=[r_cd], dma="cd")
                    for a, src in enumerate((ZC, ZS)):
                        T.op("sp", lambda e, a=a, src=src: e.dma_start(out=zc[:, a, :, :], in_=src[S:S + CT, :].rearrange("(t p) c -> p t c", p=128)),
                             w=[r_zc], dma="zc")
                    for c2 in range(2):
                        i = 0
                        for a in range(2):
                            for t in range(2):
                                T.op("pe", lambda e, c2=c2, a=a, t=t, i=i: e.matmul(pc[:, c2, 0:256], zc[:, a, t, c2 * 128:(c2 + 1) * 128], cd[:, t, a, :],
                                                                              start=(i == 0), stop=(i == 3)), r=[r_zc, r_cd], w=[r_pc[c2]])
                                i += 1
                        T.op("dve", lambda e, c2=c2: e.tensor_copy(yfc[:, c2, :], pc[:, c2, 0:256]), r=[r_pc[c2]], w=[r_yfc])
                    for c2 in range(2):
                        T.op("sp", lambda e, c2=c2: e.dma_start(out=MIXT[256 + c2 * 128:256 + (c2 + 1) * 128, S:S + CT], in_=yfc[:, c2, :]), r=[r_yfc], dma="yfc")
                T.end_phase()

        def phase_attn(l, do_ctx):
            with ExitStack() as es:
                bmt = es.enter_context(nc.sbuf_tensor(U("bmt"), [128, 5, 8, 640], BF16))
                idb = es.enter_context(nc.sbuf_tensor(U("idb2"), [128, 128], BF16))
                ktc = es.enter_context(nc.sbuf_tensor(U("ktc"), [128, 4, 256], BF16))
                vc = es.enter_context(nc.sbuf_tensor(U("vc"), [128, 2, 512], BF16))
                qt = es.enter_context(nc.sbuf_tensor(U("qt"), [128, 2, 4, 128], BF16))
                ktw = es.enter_context(nc.sbuf_tensor(U("ktw"), [128, 2, 4, 640], BF16))
                vw = es.enter_context(nc.sbuf_tensor(U("vw"), [128, 2, 5, 512], BF16))
                pm = es.enter_context(nc.sbuf_tensor(U("pm"), [128, 2, 1024], BF16))
                ptT = es.enter_context(nc.sbuf_tensor(U("ptT"), [128, 2, 7, 128], BF16))
                stt = es.enter_context(nc.sbuf_tensor(U("st"), [128, 2, 4], F32))
                osb = es.enter_context(nc.sbuf_tensor(U("osb"), [128, 2, 512], BF16))
                oT = es.enter_context(nc.sbuf_tensor(U("oT"), [128, 2, 4, 512], BF16))
                psS = es.enter_context(nc.psum_tensor(U("psS"), [128, 2, 2, 512], F32))
                psT = es.enter_context(nc.psum_tensor(U("psT"), [128, 2, 8, 128], BF16))
                psO = es.enter_context(nc.psum_tensor(U("psO"), [128, 512], F32))
                psX = es.enter_context(nc.psum_tensor(U("psX"), [128, 4, 128], BF16))
                r_bmt = T.R(); r_idb = T.R(); r_ktc = T.R(); r_vc = T.R(); r_qt = [T.R(), T.R()]; r_ktw = [T.R(), T.R()]; r_vw = [T.R(), T.R()]
                r_pm = [T.R(), T.R()]; r_ptT = [T.R(), T.R()]; r_st = [T.R(), T.R()]; r_osb = [T.R(), T.R()]; r_oT = [T.R(), T.R()]
                r_S = [T.R(), T.R()]; r_T = [T.R(), T.R()]; r_O = T.R(); r_X = T.R()
                T.op("sp", lambda e: e.dma_start(out=idb[:], in_=identb_d), w=[r_idb], dma="idb")
                for cs_ in range(5):
                    T.op("sp", lambda e, cs_=cs_: e.dma_start(out=bmt[:, cs_, :, :], in_=bm_d[l, cs_].rearrange("p (h k) -> p h k", h=8)), w=[r_bmt], dma="bmt")
                T.op("sp", lambda e: e.dma_start(out=ktc[:], in_=KT[:, S:S + CT].rearrange("(c p) t -> p c t", p=128)), w=[r_ktc], dma="ktc")
                T.op("sp", lambda e: e.dma_start(out=vc[:], in_=VV[S:S + CT, :].rearrange("(t p) c -> p t c", p=128)), w=[r_vc], dma="vc")
                for sb in range(2):
                    T.op("dve", lambda e, sb=sb: e.memset(psS[:, sb, 1, 384:512], NEG), w=[r_S[sb]])
                units = [(j, 0) for j in range(64)] + ([(64, 1), (65, 1)] if do_ctx else [])
                if os.environ.get("ATTN_UNITS"):
                    units = [u for u in units if u[0] in [int(v) for v in os.environ["ATTN_UNITS"].split(",")]]
                flat = [(ui, j, isctx, h) for ui, (j, isctx) in enumerate(units) for h in range(8)]

                def keyinfo(ui, isctx, sb):
                    sl = ui % 2
                    if not isctx:
                        return (psS[:, sb, :, :].rearrange("p a k -> p (a k)")[:, 0:896], pm[:, sb, 0:896],
                                [(0, vw, sl, b) for b in range(5)] + [(1, vc, None, b) for b in range(2)])
                    return (psS[:, sb, 0, 0:256], pm[:, sb, 0:256], [(1, vc, None, b) for b in range(2)])

                def stage1(n):
                    ui, j, isctx, h = flat[n]
                    sl = ui % 2
                    sb = n % 2
                    def loads(ui2):
                        j2, isctx2 = units[ui2]
                        sl2 = ui2 % 2
                        T.op("sp", lambda e, sl2=sl2, j2=j2: e.dma_start(out=qt[:, sl2, :, :], in_=QT[:, j2 * 128:(j2 + 1) * 128].rearrange("(c p) t -> p c t", p=128)),
                             w=[r_qt[sl2]], dma="qt%d" % sl2)
                        if not isctx2:
                            start2 = min(max(2 * j2 - 4, 0), 118)
                            T.op("sp", lambda e, sl2=sl2, start2=start2: e.dma_start(
                                out=ktw[:, sl2, :, :], in_=KT[:, start2 * 64: start2 * 64 + 640].rearrange("(c p) t -> p c t", p=128)), w=[r_ktw[sl2]], dma="ktw%d" % sl2)
                            T.op("sp", lambda e, sl2=sl2, start2=start2: e.dma_start(
                                out=vw[:, sl2, :, :], in_=VV[start2 * 64: start2 * 64 + 640, :].rearrange("(t p) c -> p t c", p=128)), w=[r_vw[sl2]], dma="vw%d" % sl2)

                    if n == 0:
                        loads(0)
                    if h == 1 and ui + 1 < len(units):
                        loads(ui + 1)
                    c = h // 2
                    p0 = (h % 2) * 64
                    qv = qt[p0:p0 + 64, sl, c, :]
                    if not isctx:
                        start = min(max(2 * j - 4, 0), 118)
                        case = (2 * j - start) // 2
                        T.op("pe", lambda e, sb=sb, qv=qv, p0=p0, c=c, sl=sl: e.matmul(psS[:, sb, 0, :], qv, ktw[p0:p0 + 64, sl, c, 0:512], start=True, stop=False),
                             r=[r_qt[sl], r_ktw[sl]], w=[r_S[sb]])
                        T.op("pe", lambda e, sb=sb, h=h, case=case: e.matmul(psS[:, sb, 0, :], idb[:], bmt[:, case, h, 0:512], start=False, stop=True),
                             r=[r_idb, r_bmt], w=[r_S[sb]])
                        T.op("pe", lambda e, sb=sb, qv=qv, p0=p0, c=c, sl=sl: e.matmul(psS[:, sb, 1, 0:128], qv, ktw[p0:p0 + 64, sl, c, 512:640], start=True, stop=False),
                             r=[r_qt[sl], r_ktw[sl]], w=[r_S[sb]])
                        T.op("pe", lambda e, sb=sb, h=h, case=case: e.matmul(psS[:, sb, 1, 0:128], idb[:], bmt[:, case, h, 512:640], start=False, stop=True),
                             r=[r_idb, r_bmt], w=[r_S[sb]])
                        T.op("pe", lambda e, sb=sb, qv=qv, p0=p0, c=c: e.matmul(psS[:, sb, 1, 128:384], qv, ktc[p0:p0 + 64, c, :], start=True, stop=True),
                             r=[r_qt[sl], r_ktc], w=[r_S[sb]])
                    else:
                        T.op("pe", lambda e, sb=sb, qv=qv, p0=p0, c=c: e.matmul(psS[:, sb, 0, 0:256], qv, ktc[p0:p0 + 64, c, :], start=True, stop=True),
                             r=[r_qt[sl], r_ktc], w=[r_S[sb]])
                    sview, pview, _ = keyinfo(ui, isctx, sb)
                    T.op("dve", lambda e, sb=sb, sview=sview: e.tensor_reduce(out=stt[:, sb, 0:1], in_=sview, axis=AX.X, op=ALU.max),
                         r=[r_S[sb]], w=[r_st[sb]])
                    T.op("dve", lambda e, sb=sb: e.tensor_scalar(out=stt[:, sb, 1:2], in0=stt[:, sb, 0:1], scalar1=-1.0, scalar2=None, op0=ALU.mult),
                         r=[r_st[sb]], w=[r_st[sb]])
                    T.op("act", lambda e, sb=sb, sview=sview, pview=pview: e.activation(out=pview, in_=sview, func=AF.Exp, bias=stt[:, sb, 1:2],
                                                                                  accum_out=stt[:, sb, 2:3]), r=[r_S[sb], r_st[sb]], w=[r_pm[sb], r_st[sb]])

                def stage2(n):
                    ui, j, isctx, h = flat[n]
                    sl = ui % 2
                    sb = n % 2
                    _, _, blocks = keyinfo(ui, isctx, sb)
                    nb = len(blocks)
                    for bi in range(nb):
                        T.op("pe", lambda e, sb=sb, bi=bi: e.transpose(psT[:, sb, bi, :], pm[:, sb, bi * 128:(bi + 1) * 128], idb[:]),
                             r=[r_pm[sb], r_idb], w=[r_T[sb]])
                    T.op("dve", lambda e, sb=sb, nb=nb: e.tensor_copy(ptT[:, sb, 0:nb, :], psT[:, sb, 0:nb, :]), r=[r_T[sb]], w=[r_ptT[sb]])
                    for bi, (isc, vbuf, vsl, b) in enumerate(blocks):
                        rv = vbuf[:, b, h * 64:(h + 1) * 64] if isc else vbuf[:, vsl, b, h * 64:(h + 1) * 64]
                        T.op("pe", lambda e, sb=sb, bi=bi, rv=rv, h=h, nb=nb: e.matmul(psO[:, h * 64:(h + 1) * 64], ptT[:, sb, bi, :], rv,
                                                                                 start=(bi == 0), stop=(bi == nb - 1)),
                             r=[r_ptT[sb], r_vc] + ([r_vw[sl]] if not isctx else []), w=[r_O])
                    T.op("dve", lambda e, sb=sb: e.reciprocal(out=stt[:, sb, 3:4], in_=stt[:, sb, 2:3]), r=[r_st[sb]], w=[r_st[sb]])
                    T.op("act", lambda e, sb=sb, sl=sl, h=h: e.activation(out=osb[:, sl, h * 64:(h + 1) * 64], in_=psO[:, h * 64:(h + 1) * 64],
                                                                        func=AF.Identity, scale=stt[:, sb, 3:4]), r=[r_O, r_st[sb]], w=[r_osb[sl]])
                    if h != 7:
                        return
                    if isctx:
                        grp, gi, gn_ = 16, j - 64, 2
                    else:
                        grp, gi, gn_ = j // 4, j % 4, 4
                    osl = grp % 2
                    for c in range(4):
                        T.op("pe", lambda e, sl=sl, c=c: e.transpose(psX[:, c, :], osb[:, sl, c * 128:(c + 1) * 128], idb[:]), r=[r_osb[sl], r_idb], w=[r_X])
                    T.op("dve", lambda e, osl=osl, gi=gi: e.tensor_copy(oT[:, osl, :, gi * 128:(gi + 1) * 128], psX[:, :, :]), r=[r_X], w=[r_oT[osl]])
                    if gi == gn_ - 1:
                        tok0 = (grp * 4 * 128) if not isctx else S
                        if isctx:
                            for c in range(4):
                                T.op("sp", lambda e, osl=osl, tok0=tok0, gn_=gn_, c=c: e.dma_start(
                                    out=MIXT[512 + c * 128:512 + (c + 1) * 128, tok0:tok0 + gn_ * 128], in_=oT[:, osl, c, 0:gn_ * 128]),
                                    r=[r_oT[osl]], dma="oT%d" % osl)
                        else:
                            T.op("sp", lambda e, osl=osl, tok0=tok0, gn_=gn_: e.dma_start(
                                out=MIXT[512:1024, tok0:tok0 + gn_ * 128].rearrange("(c p) t -> p c t", p=128), in_=oT[:, osl, :, 0:gn_ * 128]),
                                r=[r_oT[osl]], dma="oT%d" % osl)

                stage1(0)
                for n in range(len(flat)):
                    if n + 1 < len(flat):
                        stage1(n + 1)
                    stage2(n)
                T.end_phase()

        def phase_C(l, do_ctx):
            with ExitStack() as es:
                wo = es.enter_context(nc.sbuf_tensor(U("wo"), [128, 8, D], BF16))
                g1 = es.enter_context(nc.sbuf_tensor(U("g1"), [128, 2, D], F32))
                mx = es.enter_context(nc.sbuf_tensor(U("mx"), [128, 2, 8, 512], BF16))
                xc = es.enter_context(nc.sbuf_tensor(U("xc"), [128, 2, 4, D], F32))
                tmp = es.enter_context(nc.sbuf_tensor(U("tmp"), [128, 2, 512], F32))
                po = es.enter_context(nc.psum_tensor(U("po"), [128, 4, 512], F32))
                r_wo = T.R(); r_g1 = T.R(); r_mx = [T.R(), T.R()]; r_xc = [T.R(), T.R()]; r_tmp = [T.R(), T.R()]; r_po = [T.R() for _ in range(4)]
                for ck in range(8):
                    T.op("pool", lambda e, ck=ck: e.dma_start(out=wo[:, ck, :], in_=w_out[l, ck * 128:(ck + 1) * 128, :]), w=[r_wo], dma="wo")
                for k in range(2):
                    T.op("sp", lambda e, k=k: e.dma_start(out=g1[:, k, :], in_=MOD[l, k][:, 2 * D:3 * D]), w=[r_g1], dma="g1")
                sts = super_tiles() if do_ctx else super_tiles()[:16]
                it = 0
                def loadC(si):
                    t0, ntl, k = sts[si]
                    sl = si % 2
                    ntok = ntl * 128
                    T.op("sp", lambda e, sl=sl, t0=t0, ntok=ntok: e.dma_start(
                        out=mx[:, sl, :, 0:ntok], in_=MIXT[0:D, t0 * 128:t0 * 128 + ntok].rearrange("(c p) t -> p c t", p=128)), w=[r_mx[sl]], dma="mx%d" % sl)
                    T.op("sp", lambda e, sl=sl, t0=t0, ntl=ntl: e.dma_start(
                        out=xc[:, sl, 0:ntl, :], in_=XS[t0 * 128:(t0 + ntl) * 128, :].rearrange("(t p) d -> p t d", p=128)), w=[r_xc[sl]], dma="xc%d" % sl)

                loadC(0)
                for si, (t0, ntl, k) in enumerate(sts):
                    sl = si % 2
                    ntok = ntl * 128
                    if si + 1 < len(sts):
                        loadC(si + 1)
                    for t in range(ntl):
                        for hf in range(2):
                            pb = it % 4
                            tb = it % 2
                            it += 1
                            for ck in range(8):
                                T.op("pe", lambda e, pb=pb, ck=ck, sl=sl, t=t, hf=hf: e.matmul(po[:, pb, :], mx[:, sl, ck, t * 128:(t + 1) * 128],
                                                                                         wo[:, ck, hf * 512:(hf + 1) * 512], start=(ck == 0), stop=(ck == 7)),
                                     r=[r_mx[sl], r_wo], w=[r_po[pb]])
                            T.op("dve", lambda e, pb=pb, tb=tb, k=k, hf=hf: e.tensor_tensor(out=tmp[:, tb, :], in0=po[:, pb, :], in1=g1[:, k, hf * 512:(hf + 1) * 512], op=ALU.mult),
                                 r=[r_po[pb], r_g1], w=[r_tmp[tb]])
                            T.op("pool", lambda e, tb=tb, sl=sl, t=t, hf=hf: e.tensor_tensor(out=xc[:, sl, t, hf * 512:(hf + 1) * 512], in0=xc[:, sl, t, hf * 512:(hf + 1) * 512],
                                                                                        in1=tmp[:, tb, :], op=ALU.add), r=[r_tmp[tb], r_xc[sl]], w=[r_xc[sl]])
                    T.op("sp", lambda e, sl=sl, t0=t0, ntl=ntl: e.dma_start(
                        out=XS[t0 * 128:(t0 + ntl) * 128, :].rearrange("(t p) d -> p t d", p=128), in_=xc[:, sl, 0:ntl, :]), r=[r_xc[sl]], dma="xc%d" % sl)
                T.end_phase()

        def phase_D(l, do_ctx):
            with ExitStack() as es:
                w1s = es.enter_context(nc.sbuf_tensor(U("w1s"), [128, 8, DFF], BF16))
                w3s = es.enter_context(nc.sbuf_tensor(U("w3s"), [128, 8, DFF], BF16))
                w2s = es.enter_context(nc.sbuf_tensor(U("w2s"), [128, NF, D], BF16))
                idb = es.enter_context(nc.sbuf_tensor(U("idb3"), [128, 128], BF16))
                md = es.enter_context(nc.sbuf_tensor(U("md"), [128, 3, D], F32))
                xd = es.enter_context(nc.sbuf_tensor(U("xd"), [128, 4, D], F32))
                ssd = es.enter_context(nc.sbuf_tensor(U("ssd"), [128, 4], F32))
                hbd = es.enter_context(nc.sbuf_tensor(U("hbd"), [128, 2, D], BF16))
                hTd = es.enter_context(nc.sbuf_tensor(U("hTd"), [128, 8, 512], BF16))
                sg = es.enter_context(nc.sbuf_tensor(U("sg"), [128, 2, 512], F32))
                gT = es.enter_context(nc.sbuf_tensor(U("gT"), [128, NF, 512], BF16))
                tmpd = es.enter_context(nc.sbuf_tensor(U("tmpd"), [128, 2, 512], F32))
                ptd = es.enter_context(nc.psum_tensor(U("ptd"), [128, 8, 128], BF16))
                pa = es.enter_context(nc.psum_tensor(U("pa"), [128, 2, 2, 512], F32))
                pod = es.enter_context(nc.psum_tensor(U("pod"), [128, 2, 512], F32))
                r_w1 = T.R(); r_w3 = T.R(); r_w2 = T.R(); r_idb = T.R(); r_md = T.R(); r_xd = T.R(); r_ss = T.R()
                r_hb = [T.R(), T.R()]; r_hT = T.R(); r_sg = [T.R(), T.R()]; r_gT = T.R(); r_tmp = [T.R(), T.R()]
                hf_ = tmpd[:].rearrange("p a n -> p (a n)")
                r_ptd = T.R(); r_pa = [T.R(), T.R()]; r_pod = [T.R(), T.R()]
                T.op("sp", lambda e: e.dma_start(out=idb[:], in_=identb_d), w=[r_idb], dma="idb")
                for ck in range(8):
                    T.op("pool", lambda e, ck=ck: e.dma_start(out=w1s[:, ck, :], in_=w1[l, ck * 128:(ck + 1) * 128, :]), w=[r_w1], dma="w1")
                    T.op("pool", lambda e, ck=ck: e.dma_start(out=w3s[:, ck, :], in_=w3[l, ck * 128:(ck + 1) * 128, :]), w=[r_w3], dma="w3")
                for f in range(NF):
                    T.op("pool", lambda e, f=f: e.dma_start(out=w2s[:, f, :], in_=w2[l, f * 128:(f + 1) * 128, :]), w=[r_w2], dma="w2")
                sts = super_tiles() if do_ctx else super_tiles()[:16]
                it = 0
                for si, (t0, ntl, k) in enumerate(sts):
                    ntok = ntl * 128
                    if si == 0 or k != sts[si - 1][2]:
                        T.op("sp", lambda e, k=k: e.dma_start(out=md[:], in_=MOD[l, k][:, 3 * D:6 * D].rearrange("p (a d) -> p a d", a=3)), w=[r_md], dma="md")
                    T.op("sp", lambda e, t0=t0, ntl=ntl: e.dma_start(out=xd[:, 0:ntl, :], in_=XS[t0 * 128:(t0 + ntl) * 128, :].rearrange("(t p) d -> p t d", p=128)),
                         w=[r_xd], dma="xd")
                    for t in range(ntl):
                        T.op("act", lambda e, t=t: e.activation(out=hf_, in_=xd[:, t, :], func=AF.Square, accum_out=ssd[:, t:t + 1]), r=[r_xd], w=[r_tmp[0], r_tmp[1], r_ss])
                    T.op("dve", lambda e: e.tensor_scalar(out=ssd[:], in0=ssd[:], scalar1=1.0 / D, scalar2=EPS, op0=ALU.mult, op1=ALU.add), r=[r_ss], w=[r_ss])
                    T.op("act", lambda e: e.activation(out=ssd[:], in_=ssd[:], func=AF.Sqrt), r=[r_ss], w=[r_ss])
                    T.op("dve", lambda e: e.reciprocal(out=ssd[:], in_=ssd[:]), r=[r_ss], w=[r_ss])
                    for t in range(ntl):
                        hs = t % 2
                        T.op("dve", lambda e, t=t, k=k: e.scalar_tensor_tensor(out=hf_, in0=xd[:, t, :], scalar=ssd[:, t:t + 1], in1=md[:, 1, :],
                                                                              op0=ALU.mult, op1=ALU.mult), r=[r_xd, r_ss, r_md], w=[r_tmp[0], r_tmp[1]])
                        T.op("dve", lambda e, hs=hs, k=k: e.tensor_tensor(out=hbd[:, hs, :], in0=hf_, in1=md[:, 0, :], op=ALU.add), r=[r_tmp[0], r_tmp[1], r_md], w=[r_hb[hs]])
                        for ck in range(8):
                            T.op("pe", lambda e, hs=hs, ck=ck: e.transpose(ptd[:, ck, :], hbd[:, hs, ck * 128:(ck + 1) * 128], idb[:]), r=[r_hb[hs], r_idb], w=[r_ptd])
                        T.op("act", lambda e, t=t: e.copy(out=hTd[:, :, t * 128:(t + 1) * 128], in_=ptd[:, :, :]), r=[r_ptd], w=[r_hT])
                    for f in range(NF):
                        pb = f % 2
                        for wi, (ws, rw) in enumerate(((w1s, r_w1), (w3s, r_w3))):
                            for ck in range(8):
                                T.op("pe", lambda e, pb=pb, wi=wi, ws=ws, ck=ck, f=f, ntok=ntok: e.matmul(pa[:, pb, wi, 0:ntok], ws[:, ck, f * 128:(f + 1) * 128], hTd[:, ck, 0:ntok],
                                                                                                    start=(ck == 0), stop=(ck == 7)), r=[rw, r_hT], w=[r_pa[pb]])
                        T.op("act", lambda e, pb=pb, ntok=ntok: e.activation(out=sg[:, pb, 0:ntok], in_=pa[:, pb, 0, 0:ntok], func=AF.Silu), r=[r_pa[pb]], w=[r_sg[pb]])
                        T.op("dve", lambda e, pb=pb, f=f, ntok=ntok: e.tensor_tensor(out=gT[:, f, 0:ntok], in0=sg[:, pb, 0:ntok], in1=pa[:, pb, 1, 0:ntok], op=ALU.mult),
                             r=[r_sg[pb], r_pa[pb]], w=[r_gT])
                    for t in range(ntl):
                        for hfi in range(2):
                            pb = it % 2
                            it += 1
                            for f in range(NF):
                                T.op("pe", lambda e, pb=pb, f=f, t=t, hfi=hfi: e.matmul(pod[:, pb, :], gT[:, f, t * 128:(t + 1) * 128], w2s[:, f, hfi * 512:(hfi + 1) * 512],
                                                                                   start=(f == 0), stop=(f == NF - 1)), r=[r_gT, r_w2], w=[r_pod[pb]])
                            T.op("dve", lambda e, pb=pb, k=k, hfi=hfi: e.tensor_tensor(out=tmpd[:, pb, :], in0=pod[:, pb, :], in1=md[:, 2, hfi * 512:(hfi + 1) * 512], op=ALU.mult),
                                 r=[r_pod[pb], r_md], w=[r_tmp[pb]])
                            T.op("pool", lambda e, pb=pb, t=t, hfi=hfi: e.tensor_tensor(out=xd[:, t, hfi * 512:(hfi + 1) * 512], in0=xd[:, t, hfi * 512:(hfi + 1) * 512],
                                                                                   in1=tmpd[:, pb, :], op=ALU.add), r=[r_tmp[pb], r_xd], w=[r_xd])
                    T.op("sp", lambda e, t0=t0, ntl=ntl: e.dma_start(out=XS[t0 * 128:(t0 + ntl) * 128, :].rearrange("(t p) d -> p t d", p=128), in_=xd[:, 0:ntl, :]),
                         r=[r_xd], dma="xd")
                T.end_phase()

        def phase_final():
            with ExitStack() as es:
                fg = es.enter_context(nc.sbuf_tensor(U("fg"), [128, D], F32))
                xf = es.enter_context(nc.sbuf_tensor(U("xf"), [128, 2, 4, D], F32))
                jk = es.enter_context(nc.sbuf_tensor(U("jk"), [128, D], F32))
                sf = es.enter_context(nc.sbuf_tensor(U("sf"), [128, 8], F32))
                r_fg = T.R(); r_xf = [T.R(), T.R()]; r_jk = T.R(); r_sf = [T.R(), T.R()]
                T.op("sp", lambda e: e.dma_start(out=fg[:], in_=bcast_rows(fing)), w=[r_fg], dma="fg")
                for si in range(16):
                    sl = si % 2
                    T.op("sp", lambda e, sl=sl, si=si: e.dma_start(out=xf[:, sl, :, :], in_=XS[si * 512:(si + 1) * 512, :].rearrange("(t p) d -> p t d", p=128)),
                         w=[r_xf[sl]], dma="xf%d" % sl)
                    for t in range(4):
                        T.op("act", lambda e, sl=sl, t=t: e.activation(out=jk[:], in_=xf[:, sl, t, :], func=AF.Square, accum_out=sf[:, sl * 4 + t: sl * 4 + t + 1]),
                             r=[r_xf[sl]], w=[r_jk, r_sf[sl]])
                    T.op("dve", lambda e, sl=sl: e.tensor_scalar(out=sf[:, sl * 4:sl * 4 + 4], in0=sf[:, sl * 4:sl * 4 + 4], scalar1=1.0 / D, scalar2=EPS,
                                                                 op0=ALU.mult, op1=ALU.add), r=[r_sf[sl]], w=[r_sf[sl]])
                    T.op("act", lambda e, sl=sl: e.activation(out=sf[:, sl * 4:sl * 4 + 4], in_=sf[:, sl * 4:sl * 4 + 4], func=AF.Sqrt), r=[r_sf[sl]], w=[r_sf[sl]])
                    T.op("dve", lambda e, sl=sl: e.reciprocal(out=sf[:, sl * 4:sl * 4 + 4], in_=sf[:, sl * 4:sl * 4 + 4]), r=[r_sf[sl]], w=[r_sf[sl]])
                    for t in range(4):
                        T.op("dve", lambda e, sl=sl, t=t: e.scalar_tensor_tensor(
                            out=xf[:, sl, t, :], in0=xf[:, sl, t, :], scalar=sf[:, sl * 4 + t: sl * 4 + t + 1], in1=fg[:], op0=ALU.mult, op1=ALU.mult),
                            r=[r_sf[sl], r_fg, r_xf[sl]], w=[r_xf[sl]])
                    T.op("sp", lambda e, sl=sl, si=si: e.dma_start(out=out[si * 512:(si + 1) * 512, :].rearrange("(t p) d -> p t d", p=128), in_=xf[:, sl, :, :]),
                         r=[r_xf[sl]], dma="xf%d" % sl)
                T.end_phase()

        on = (lambda n: only is None or n in only)
        if on("pro"):
            phase_prologue()
        for l in range(nlayers):
            do_ctx = (l < nlayers - 1) or force_ctx
            cm = (lambda n: do_ctx and (ctx_mask is None or n in ctx_mask))
            if on("A"):
                phase_A(l)
            if on("pool"):
                phase_pool(l, cm("pool"))
            if on("f1"):
                phase_f1()
            if on("f2"):
                phase_f2(cm("f2"))
            if on("attn"):
                phase_attn(l, cm("attn"))
            if on("C"):
                phase_C(l, cm("C"))
            if on("D"):
                phase_D(l, cm("D"))
        if on("fin"):
            phase_final()
        print('icount', T.icount, 'ninst', nc.n_instructions if not callable(nc.n_instructions) else nc.n_instructions()); print('tracker counts', T.ecount, {k: v for k, v in T.dcount.items() if v > 50}, 'nsem', len(T.dsem) + 5, 'total ops', T.total)
    return nc


def _bf(a):
    return np.ascontiguousarray(a.astype(ml_dtypes.bfloat16))


def host_constants(nat_bias, fourier_w, pool_scale):
    c = {}
    c["identb"] = _bf(np.eye(128, dtype=np.float32))
    a = np.arange(64)
    ang = 2 * np.pi * np.outer(a, a) / 64.0
    C64 = np.cos(ang); S64 = np.sin(ang)
    ccbd = np.zeros((2, 128, 128), np.float64)
    for i, m in enumerate((C64, S64)):
        ccbd[i, :64, :64] = m.T
        ccbd[i, 64:, 64:] = m.T
    c["ccbd"] = _bf(ccbd)
    pb = np.zeros((4, 3, 3, 128, 128), np.float64)
    Sp = 384
    t = np.arange(Sp)
    for g, w in enumerate((2, 4, 8, 16)):
        lo = np.clip(t - w // 2, 0, Sp); hi = np.clip(t + w - w // 2, 0, Sp)
        B = np.zeros((Sp, Sp))
        for tt in range(Sp):
            B[tt, lo[tt]:hi[tt]] = 1.0 / (hi[tt] - lo[tt])
            B[tt, tt] -= 1.0
        for kind in range(3):
            for o in (-1, 0, 1):
                st = kind + o
                if 0 <= st < 3:
                    pb[g, kind, o + 1] = B[kind * 128:(kind + 1) * 128, st * 128:(st + 1) * 128].T
    c["pb"] = _bf(pb)
    nrm = 1.0 / np.sqrt(S * 64.0)
    ph = 2 * np.pi * np.outer(a, a) / 64.0
    f1 = np.zeros((128, 128))
    f1[:64, :64] = np.cos(ph) * nrm; f1[64:, :64] = -np.sin(ph) * nrm
    f1[:64, 64:] = np.sin(ph) * nrm; f1[64:, 64:] = np.cos(ph) * nrm
    c["f1"] = _bf(f1)
    s1 = np.arange(128)[:, None, None]; k2 = np.arange(64)[None, :, None]; k1 = np.arange(128)[None, None, :]
    psi = 2 * np.pi * ((s1 * (64 * k1 + k2)) % S) / float(S)
    m2 = np.stack([np.cos(psi), -np.sin(psi)], axis=2)
    c["m2"] = _bf(m2.reshape(128, -1))
    nrc = 1.0 / np.sqrt(CT * 64.0)
    p = np.arange(128)[:, None, None]; tt = np.arange(2)[None, :, None]; kk = np.arange(CT)[None, None, :]
    th = 2 * np.pi * (((tt * 128 + p) * kk) % CT) / float(CT)
    cd = np.stack([np.cos(th) * nrc, -np.sin(th) * nrc], axis=2)
    c["cd"] = _bf(cd.reshape(128, -1))
    bm = np.full((L, 5, 8, 128, 640), NEG, np.float32)
    q = np.arange(128); qr = q // 64; qc = q % 64
    key = np.arange(640); ki = key // 64; kc = key % 64
    cs = np.clip(qc - 8, 0, 48)
    for case, j in enumerate((0, 1, 2, 62, 63)):
        start = min(max(2 * j - 4, 0), 118)
        r = 2 * j + qr
        rs = np.clip(r - 4, 0, 120)
        krow = start + ki
        ok = (krow[None, :] >= rs[:, None]) & (krow[None, :] < rs[:, None] + 8) & \
             (kc[None, :] >= cs[:, None]) & (kc[None, :] < cs[:, None] + 16)
        dr = np.clip(krow[None, :] - r[:, None] + 7, 0, 14)
        dc = np.clip(kc[None, :] - qc[:, None] + 15, 0, 30)
        for l in range(L):
            g = nat_bias[l][:, dr, dc]
            bm[l, case] = np.where(ok[None], g, NEG)
    c["bm"] = _bf(np.ascontiguousarray(bm.transpose(0, 1, 3, 2, 4)).reshape(L, 5, 128, 8 * 640))
    fwbd = np.zeros((L, 2, 128, 128), np.float32)
    for l in range(L):
        for c2 in range(2):
            fwbd[l, c2, :64, :64] = fourier_w[l, 2 * c2]
            fwbd[l, c2, 64:, 64:] = fourier_w[l, 2 * c2 + 1]
    c["fwbd"] = fwbd
    c["pscol"] = np.ascontiguousarray(pool_scale.reshape(L, 4, 64).transpose(0, 2, 1))
    return c


_NC_CACHE = {}


def kernel(x, c, ctx, c_ctx, w_ada, b_ada, norm1_g, w_in, pool_w, pool_scale, fourier_w,
           nat_bias, w_out, norm2_g, w_ffn1, w_ffn3, w_ffn2, final_g):
    f = lambda a: np.ascontiguousarray(np.asarray(a, dtype=np.float32))
    x = f(x); c = f(c); ctx = f(ctx); c_ctx = f(c_ctx)
    consts = host_constants(f(nat_bias), f(fourier_w), f(pool_scale))
    shared = {
        "w_ada": f(w_ada), "b_ada": f(b_ada), "norm1_g": f(norm1_g), "norm2_g": f(norm2_g), "w_in": f(w_in),
        "pool_w": f(pool_w), "w_out": f(w_out), "w_ffn1": f(w_ffn1), "w_ffn3": f(w_ffn3), "w_ffn2": f(w_ffn2),
        "final_g": f(final_g).reshape(1, D),
    }
    shared.update(consts)
    if "nc" not in _NC_CACHE:
        _NC_CACHE["nc"] = build()
    nc = _NC_CACHE["nc"]
    in_maps = []
    for core in range(NCORES):
        b = core % 4
        ccm = np.concatenate([c[b].reshape(8, 128).T, c_ctx.reshape(8, 128).T], axis=1)
        m = dict(shared)
        m["x"] = x[b]
        m["ctx"] = ctx[b]
        m["cc"] = np.ascontiguousarray(ccm)
        in_maps.append(m)
    res = run_bass_kernel_spmd(nc, in_maps, core_ids=list(range(NCORES)))
    return np.stack([np.asarray(res.results[b]["out"], dtype=np.float32) for b in range(4)], axis=0)
```

```python
import bisect
import os
from contextlib import ExitStack

import numpy as np
import ml_dtypes

import concourse.bass as bass
import concourse.mybir as mybir
from concourse.bass_utils import run_bass_kernel_spmd

F32 = mybir.dt.float32
BF16 = mybir.dt.bfloat16
AF = mybir.ActivationFunctionType
ALU = mybir.AluOpType
AX = mybir.AxisListType

D = 1024
L = 4
S = 8192
CT = 256
NTOK = S + CT
NT = NTOK // 128
DFF = 2816
NF = DFF // 128
EPS = 1e-6
WCOLS = 2304
NEG = -1e30
NCORES = 4


class Res:
    __slots__ = ("name", "w", "rs")

    def __init__(self, name=""):
        self.name = name
        self.w = None
        self.rs = []


class Op:
    __slots__ = ("eng", "fn", "dma", "gidx", "deps", "signal", "cnt", "seq")


ENGS = ["pe", "act", "dve", "pool", "sp"]


class Tracker:
    def __init__(self, nc, stack):
        self.nc = nc
        self.stack = stack
        self.esem = {e: stack.enter_context(nc.semaphore("s_" + e)) for e in ENGS}
        self.ecount = {e: 0 for e in ENGS}
        self.dsem = {}
        self.dcount = {}
        self.ops = []
        self.allres = []
        self.total = 0

    def R(self, name=""):
        r = Res(name)
        self.allres.append(r)
        return r

    def op(self, eng, fn, r=(), w=(), dma=None):
        o = Op()
        o.eng = eng
        o.fn = fn
        o.dma = dma
        o.gidx = len(self.ops)
        o.signal = False
        deps = set()
        for b in r:
            if b.w is not None:
                deps.add(b.w)
        for b in w:
            if b.w is not None:
                deps.add(b.w)
            deps.update(b.rs)
        for b in r:
            b.rs.append(o)
        for b in w:
            b.w = o
            b.rs = []
        deps.discard(o)
        o.deps = deps
        if dma is not None and dma not in self.dsem:
            self.dsem[dma] = self.stack.enter_context(self.nc.semaphore("d_" + str(dma)))
            self.dcount[dma] = 0
        self.ops.append(o)
        return o

    def end_phase(self):
        nc = self.nc
        ops = self.ops
        dma_ops = [o for o in ops if o.dma is not None]
        fin = Op()
        fin.eng = "sp"; fin.fn = None; fin.dma = None; fin.gidx = len(ops); fin.signal = False
        fin.deps = set(dma_ops)
        ops.append(fin)
        per = {e: [] for e in ENGS}
        for o in ops:
            o.seq = len(per[o.eng])
            per[o.eng].append(o)
        dkeys = {}
        for o in dma_ops:
            dkeys.setdefault(o.dma, []).append(o.gidx)
        plan = {}
        for o in ops:
            cdeps = {}
            ddeps = {}
            for d in o.deps:
                if d.dma is not None:
                    n = bisect.bisect_left(dkeys[d.dma], o.gidx)
                    ddeps[d.dma] = 16 * (self.dcount[d.dma] + n)
                else:
                    if d.eng == "pe" and o.eng == "pe":
                        continue
                    if d.eng not in cdeps or cdeps[d.eng].seq < d.seq:
                        cdeps[d.eng] = d
            for d in cdeps.values():
                d.signal = True
            plan[o.gidx] = (cdeps, ddeps)
        for e in ENGS:
            c = self.ecount[e]
            for o in per[e]:
                if o.signal:
                    c += 1
                o.cnt = c
            self.ecount[e] = c
        for k, lst in dkeys.items():
            self.dcount[k] += len(lst)
        esem = self.esem
        dsem = self.dsem

        def run(eng_name, eh):
            known = {}
            self.icount = getattr(self, "icount", {})
            for o in per[eng_name]:
                self.icount[eng_name] = self.icount.get(eng_name, 0) + 1 + sum(1 for d in plan[o.gidx][0].values() if known.get(("e", d.eng), -1) < d.cnt) + sum(1 for k, v in plan[o.gidx][1].items() if known.get(("d", k), -1) < v)
                cdeps, ddeps = plan[o.gidx]
                for d in cdeps.values():
                    if known.get(("e", d.eng), -1) < d.cnt:
                        eh.wait_ge(esem[d.eng], d.cnt)
                        known[("e", d.eng)] = d.cnt
                for k, v in ddeps.items():
                    if known.get(("d", k), -1) < v:
                        eh.wait_ge(dsem[k], v)
                        known[("d", k)] = v
                if o.fn is None:
                    continue
                ins = o.fn(eh)
                if o.dma is not None:
                    ins.then_inc(dsem[o.dma], 16)
                elif o.signal:
                    ins.then_inc(esem[eng_name], 1)

        with nc.Block() as block:
            @block.tensor
            def _(t):
                run("pe", t)

            @block.scalar
            def _(t):
                run("act", t)

            @block.vector
            def _(t):
                run("dve", t)

            @block.gpsimd
            def _(t):
                run("pool", t)

            @block.sync
            def _(t):
                run("sp", t)
        nc.all_engine_barrier()
        self.total += len(ops)
        self.ops = []
        for r_ in self.allres:
            r_.w = None
            r_.rs = []
        self.allres = []


_UC = [0]


def U(name):
    _UC[0] += 1
    return "%s_%d" % (name, _UC[0])


def bcast_rows(ap2d, nparts=128):
    n = ap2d.shape[-1]
    return bass.AP(tensor=ap2d.tensor, offset=ap2d.offset, ap=[[0, nparts], [1, n]])


def super_tiles():
    st = [(4 * i, 4, 0) for i in range(16)]
    st.append((64, 2, 1))
    return st


def build(nlayers=L, dbg=False, force_ctx=False, ctx_mask=None, only=None):
    nc = bass.Bass("TRN2", target_bir_lowering=False)

    def din(name, shape, dt=F32):
        return nc.dram_tensor(name, list(shape), dt, kind="ExternalInput").ap()

    def dscr(name, shape, dt):
        return nc.dram_tensor(name, list(shape), dt, kind=("ExternalOutput" if dbg else "Internal")).ap()

    x_in = din("x", [S, D])
    ctx_in = din("ctx", [CT, D])
    cc = din("cc", [128, 16])
    w_ada = din("w_ada", [L, D, 6 * D])
    b_ada = din("b_ada", [L, 6 * D])
    n1g = din("norm1_g", [L, D])
    n2g = din("norm2_g", [L, D])
    w_in = din("w_in", [L, D, 2048])
    pool_w = din("pool_w", [L, 4, 64, 64])
    pscol = din("pscol", [L, 64, 4])
    fwbd = din("fwbd", [L, 2, 128, 128])
    w_out = din("w_out", [L, D, D])
    w1 = din("w_ffn1", [L, D, DFF])
    w3 = din("w_ffn3", [L, D, DFF])
    w2 = din("w_ffn2", [L, DFF, D])
    fing = din("final_g", [1, D])
    identb_d = din("identb", [128, 128], BF16)
    ccbd_d = din("ccbd", [2, 128, 128], BF16)
    pb_d = din("pb", [4, 3, 3, 128, 128], BF16)
    f1_d = din("f1", [128, 128], BF16)
    m2_d = din("m2", [128, 64 * 2 * 128], BF16)
    cd_d = din("cd", [128, 2 * 2 * 256], BF16)
    bm_d = din("bm", [L, 5, 128, 8 * 640], BF16)
    out = nc.dram_tensor("out", [S, D], F32, kind="ExternalOutput").ap()

    XS = dscr("XS", [NTOK, D], F32)
    MOD = dscr("MOD", [L, 2, 128, 6 * D], F32)
    QT = dscr("QT", [512, NTOK], BF16)
    KT = dscr("KT", [512, NTOK], BF16)
    VV = dscr("VV", [NTOK, 512], BF16)
    UP = dscr("UP", [NTOK, 256], BF16)
    ZC = dscr("ZC", [NTOK, 256], BF16)
    ZS = dscr("ZS", [NTOK, 256], BF16)
    ABd = dscr("ABd", [128, 128 * 256], BF16)
    MIXT = dscr("MIXT", [D + 128, NTOK], BF16)

    with ExitStack() as stack:
        T = Tracker(nc, stack)

        def phase_prologue():
            with ExitStack() as es:
                wada = es.enter_context(nc.sbuf_tensor(U("wada"), [128, 8, 6 * D], BF16))
                bias = es.enter_context(nc.sbuf_tensor(U("bias"), [128, 6 * D], F32))
                m6 = es.enter_context(nc.sbuf_tensor(U("m6"), [128, 6 * D], F32))
                gn = es.enter_context(nc.sbuf_tensor(U("gn"), [128, 2, D], F32))
                ccs = es.enter_context(nc.sbuf_tensor(U("ccs"), [128, 16], F32))
                sil = es.enter_context(nc.sbuf_tensor(U("sil"), [128, 16], F32))
                ones = es.enter_context(nc.sbuf_tensor(U("ones"), [128, 128], F32))
                rep = es.enter_context(nc.sbuf_tensor(U("rep"), [128, 16, 128], BF16))
                xcp = es.enter_context(nc.sbuf_tensor(U("xcp"), [128, 2, 4 * D], F32))
                pp = es.enter_context(nc.psum_tensor(U("pp"), [128, 2, 512], F32))
                r_wada = T.R(); r_bias = T.R(); r_m6 = T.R(); r_gn = T.R(); r_cc = T.R(); r_sil = T.R()
                r_ones = T.R(); r_rep = T.R(); r_pp = [T.R(), T.R()]; r_xcp = [T.R(), T.R()]
                for i in range(17):
                    t0, ntl = (4 * i, 4) if i < 16 else (64, 2)
                    sl = i % 2
                    src = (x_in[t0 * 128:(t0 + ntl) * 128, :] if i < 16 else ctx_in[:, :])
                    T.op("sp", lambda e, sl=sl, src=src, ntl=ntl: e.dma_start(
                        out=xcp[:, sl, 0:ntl * D].rearrange("p (t d) -> p t d", t=ntl),
                        in_=src.rearrange("(t p) d -> p t d", p=128)), w=[r_xcp[sl]], dma="xcp%d" % sl)
                    T.op("sp", lambda e, sl=sl, t0=t0, ntl=ntl: e.dma_start(
                        out=XS[t0 * 128:(t0 + ntl) * 128, :].rearrange("(t p) d -> p t d", p=128),
                        in_=xcp[:, sl, 0:ntl * D].rearrange("p (t d) -> p t d", t=ntl)), r=[r_xcp[sl]], dma="xcp%d" % sl)
                T.op("sp", lambda e: e.dma_start(out=ccs[:], in_=cc), w=[r_cc], dma="cc")
                T.op("pool", lambda e: e.memset(ones[:], 1.0), w=[r_ones])
                T.op("act", lambda e: e.activation(out=sil[:], in_=ccs[:], func=AF.Silu), r=[r_cc], w=[r_sil])
                for i in range(16):
                    T.op("dve", lambda e, i=i: e.tensor_scalar(out=rep[:, i, :], in0=ones[:], scalar1=sil[:, i:i + 1],
                                                                scalar2=None, op0=ALU.mult), r=[r_ones, r_sil], w=[r_rep])
                for l in range(nlayers):
                    for ck in range(8):
                        T.op("pool", lambda e, l=l, ck=ck: e.dma_start(out=wada[:, ck, :], in_=w_ada[l, ck * 128:(ck + 1) * 128, :]),
                             w=[r_wada], dma="wada")
                    T.op("sp", lambda e, l=l: e.dma_start(out=bias[:], in_=bcast_rows(b_ada[l:l + 1, :])), w=[r_bias], dma="bias")
                    T.op("sp", lambda e, l=l: e.dma_start(out=gn[:, 0, :], in_=bcast_rows(n1g[l:l + 1, :])), w=[r_gn], dma="gn")
                    T.op("sp", lambda e, l=l: e.dma_start(out=gn[:, 1, :], in_=bcast_rows(n2g[l:l + 1, :])), w=[r_gn], dma="gn")
                    for k in range(2):
                        for n in range(12):
                            pb = n % 2
                            for ck in range(8):
                                T.op("pe", lambda e, pb=pb, k=k, ck=ck, n=n: e.matmul(
                                    pp[:, pb, :], rep[:, k * 8 + ck, :], wada[:, ck, n * 512:(n + 1) * 512],
                                    start=(ck == 0), stop=(ck == 7)), r=[r_rep, r_wada], w=[r_pp[pb]])
                            T.op("dve", lambda e, pb=pb, n=n: e.tensor_tensor(
                                out=m6[:, n * 512:(n + 1) * 512], in0=pp[:, pb, :], in1=bias[:, n * 512:(n + 1) * 512], op=ALU.add),
                                r=[r_pp[pb], r_bias], w=[r_m6])
                        for j, slot in enumerate((1, 4)):
                            T.op("dve", lambda e, j=j, slot=slot: e.scalar_tensor_tensor(
                                out=m6[:, slot * D:(slot + 1) * D], in0=m6[:, slot * D:(slot + 1) * D], scalar=1.0,
                                in1=gn[:, j, :], op0=ALU.add, op1=ALU.mult), r=[r_gn, r_m6], w=[r_m6])
                        T.op("sp", lambda e, l=l, k=k: e.dma_start(out=MOD[l, k], in_=m6[:]), r=[r_m6], dma="m6")
                T.end_phase()

        def phase_A(l):
            with ExitStack() as es:
                weff = es.enter_context(nc.sbuf_tensor(U("weff"), [128, 8, WCOLS], BF16))
                wf = es.enter_context(nc.sbuf_tensor(U("wf"), [128, 8, 256], BF16))
                wft = es.enter_context(nc.sbuf_tensor(U("wft"), [128, 2, D], BF16))
                bd = es.enter_context(nc.sbuf_tensor(U("bd"), [128, 2, 512], BF16))
                fwb = es.enter_context(nc.sbuf_tensor(U("fwb"), [128, 2, 128], BF16))
                ccb = es.enter_context(nc.sbuf_tensor(U("ccb"), [128, 2, 128], BF16))
                idb = es.enter_context(nc.sbuf_tensor(U("idb"), [128, 128], BF16))
                modt = es.enter_context(nc.sbuf_tensor(U("modt"), [128, 2, 2, D], F32))
                xt = es.enter_context(nc.sbuf_tensor(U("xt"), [128, 2, 4, D], F32))
                junk = es.enter_context(nc.sbuf_tensor(U("junk"), [128, D], F32))
                ss = es.enter_context(nc.sbuf_tensor(U("ss"), [128, 8], F32))
                hb = es.enter_context(nc.sbuf_tensor(U("hb"), [128, 2, D], BF16))
                hT = es.enter_context(nc.sbuf_tensor(U("hT"), [128, 2, 8, 512], BF16))
                qk = es.enter_context(nc.sbuf_tensor(U("qk"), [128, 2, 8, 512], BF16))
                tm = es.enter_context(nc.sbuf_tensor(U("tm"), [128, 2, 4, 1280], BF16))
                ptr = es.enter_context(nc.psum_tensor(U("ptr"), [128, 2, 8, 128], BF16))
                pq = es.enter_context(nc.psum_tensor(U("pq"), [128, 2, 512], F32))
                pt_ = es.enter_context(nc.psum_tensor(U("pt"), [128, 2, 512], F32))
                r_weff = T.R(); r_wf = T.R(); r_wft = T.R(); r_bd = T.R(); r_fwb = T.R(); r_ccb = T.R(); r_idb = T.R()
                r_mod = T.R(); r_xt = [T.R(), T.R()]; r_junk = T.R(); r_ss = [T.R(), T.R()]; r_hb = [T.R(), T.R()]
                r_hT = [T.R(), T.R()]; r_qk = [T.R(), T.R()]; r_tm = [T.R(), T.R()]
                r_ptr = [T.R(), T.R()]; r_pq = [T.R(), T.R()]; r_pt = [T.R(), T.R()]
                T.op("sp", lambda e: e.dma_start(out=idb[:], in_=identb_d), w=[r_idb], dma="idb")
                T.op("sp", lambda e: e.dma_start(out=ccb[:], in_=ccbd_d.rearrange("a p q -> p a q")), w=[r_ccb], dma="ccb")
                T.op("pool", lambda e: e.dma_start(out=fwb[:], in_=fwbd[l].rearrange("a p q -> p a q")), w=[r_fwb], dma="fwb")
                for ck in range(8):
                    rows = w_in[l, ck * 128:(ck + 1) * 128, :]
                    T.op("pool", lambda e, ck=ck, rows=rows: e.dma_start(out=weff[:, ck, 0:256], in_=rows[:, 0:256]), w=[r_weff], dma="weff")
                    T.op("pool", lambda e, ck=ck, rows=rows: e.dma_start(out=weff[:, ck, 768:WCOLS], in_=rows[:, 512:2048]), w=[r_weff], dma="weff")
                    T.op("pool", lambda e, ck=ck, rows=rows: e.dma_start(out=wf[:, ck, :], in_=rows[:, 256:512]), w=[r_wf], dma="wf")
                for k in range(2):
                    T.op("sp", lambda e, k=k: e.dma_start(out=modt[:, k, :, :], in_=MOD[l, k][:, 0:2 * D].rearrange("p (a d) -> p a d", a=2)),
                         w=[r_mod], dma="modt")
                T.op("pool", lambda e: e.memset(bd[:], 0.0), w=[r_bd])
                for c2 in range(2):
                    for cs_ in range(2):
                        pb = (c2 * 2 + cs_) % 2
                        T.op("pe", lambda e, c2=c2, cs_=cs_, pb=pb: e.matmul(pq[:, pb, 0:128], ccb[:, cs_, :], fwb[:, c2, :], start=True, stop=True),
                             r=[r_ccb, r_fwb], w=[r_pq[pb]])
                        T.op("dve", lambda e, c2=c2, cs_=cs_, pb=pb: e.tensor_copy(bd[:, c2, cs_ * 256 + c2 * 128: cs_ * 256 + (c2 + 1) * 128], pq[:, pb, 0:128]),
                             r=[r_pq[pb]], w=[r_bd])
                for ck in range(8):
                    pb = ck % 2
                    for c2 in range(2):
                        T.op("pe", lambda e, ck=ck, c2=c2, pb=pb: e.transpose(ptr[:, pb, c2, :], wf[:, ck, c2 * 128:(c2 + 1) * 128], idb[:]),
                             r=[r_wf, r_idb], w=[r_ptr[pb]])
                    T.op("dve", lambda e, ck=ck, pb=pb: e.tensor_copy(wft[:, :, ck * 128:(ck + 1) * 128], ptr[:, pb, 0:2, :]),
                         r=[r_ptr[pb]], w=[r_wft])
                for ck in range(8):
                    pb = ck % 2
                    for c2 in range(2):
                        T.op("pe", lambda e, ck=ck, c2=c2, pb=pb: e.matmul(pq[:, pb, :], wft[:, c2, ck * 128:(ck + 1) * 128], bd[:, c2, :],
                                                                        start=(c2 == 0), stop=(c2 == 1)), r=[r_wft, r_bd], w=[r_pq[pb]])
                    T.op("dve", lambda e, ck=ck, pb=pb: e.tensor_copy(weff[:, ck, 256:768], pq[:, pb, :]), r=[r_pq[pb]], w=[r_weff])
                stsA = super_tiles()

                def stageA1(si):
                    t0, ntl, k = stsA[si]
                    sl = si % 2
                    ntok = ntl * 128
                    T.op("sp", lambda e, sl=sl, t0=t0, ntl=ntl: e.dma_start(
                        out=xt[:, sl, 0:ntl, :], in_=XS[t0 * 128:(t0 + ntl) * 128, :].rearrange("(t p) d -> p t d", p=128)),
                        w=[r_xt[sl]], dma="xt%d" % sl)
                    for t in range(ntl):
                        T.op("act", lambda e, sl=sl, t=t: e.activation(out=junk[:], in_=xt[:, sl, t, :], func=AF.Square,
                                                                       accum_out=ss[:, sl * 4 + t: sl * 4 + t + 1]),
                             r=[r_xt[sl]], w=[r_junk, r_ss[sl]])
                    T.op("dve", lambda e, sl=sl: e.tensor_scalar(out=ss[:, sl * 4:sl * 4 + 4], in0=ss[:, sl * 4:sl * 4 + 4], scalar1=1.0 / D,
                                                                 scalar2=EPS, op0=ALU.mult, op1=ALU.add), r=[r_ss[sl]], w=[r_ss[sl]])
                    T.op("act", lambda e, sl=sl: e.activation(out=ss[:, sl * 4:sl * 4 + 4], in_=ss[:, sl * 4:sl * 4 + 4], func=AF.Sqrt), r=[r_ss[sl]], w=[r_ss[sl]])
                    T.op("dve", lambda e, sl=sl: e.reciprocal(out=ss[:, sl * 4:sl * 4 + 4], in_=ss[:, sl * 4:sl * 4 + 4]), r=[r_ss[sl]], w=[r_ss[sl]])
                    for t in range(ntl):
                        hs = t % 2
                        T.op("dve", lambda e, sl=sl, t=t, k=k: e.scalar_tensor_tensor(
                            out=xt[:, sl, t, :], in0=xt[:, sl, t, :], scalar=ss[:, sl * 4 + t: sl * 4 + t + 1], in1=modt[:, k, 1, :],
                            op0=ALU.mult, op1=ALU.mult), r=[r_ss[sl], r_mod, r_xt[sl]], w=[r_xt[sl]])
                        T.op("pool", lambda e, sl=sl, t=t, k=k, hs=hs: e.tensor_tensor(out=hb[:, hs, :], in0=xt[:, sl, t, :], in1=modt[:, k, 0, :], op=ALU.add),
                             r=[r_xt[sl], r_mod], w=[r_hb[hs]])
                        for ck in range(8):
                            T.op("pe", lambda e, hs=hs, ck=ck: e.transpose(ptr[:, hs, ck, :], hb[:, hs, ck * 128:(ck + 1) * 128], idb[:]),
                                 r=[r_hb[hs], r_idb], w=[r_ptr[hs]])
                        T.op("act", lambda e, sl=sl, t=t, hs=hs: e.copy(out=hT[:, sl, :, t * 128:(t + 1) * 128], in_=ptr[:, hs, :, :]),
                             r=[r_ptr[hs]], w=[r_hT[sl]])
                def stageA2(si):
                    t0, ntl, k = stsA[si]
                    sl = si % 2
                    ntok = ntl * 128
                    for c in range(8):
                        pb = c % 2
                        col0 = 768 + c * 128
                        for ck in range(8):
                            T.op("pe", lambda e, pb=pb, ck=ck, col0=col0, sl=sl, ntok=ntok: e.matmul(
                                pq[:, pb, 0:ntok], weff[:, ck, col0:col0 + 128], hT[:, sl, ck, 0:ntok], start=(ck == 0), stop=(ck == 7)),
                                r=[r_weff, r_hT[sl]], w=[r_pq[pb]])
                        T.op("act", lambda e, pb=pb, c=c, sl=sl, ntok=ntok: e.activation(
                            out=qk[:, sl, c, 0:ntok], in_=pq[:, pb, 0:ntok], func=AF.Identity, scale=(0.125 if c < 4 else 1.0)),
                            r=[r_pq[pb]], w=[r_qk[sl]])
                    T.op("sp", lambda e, sl=sl, t0=t0, ntok=ntok: e.dma_start(
                        out=QT[:, t0 * 128:t0 * 128 + ntok].rearrange("(c p) t -> p c t", p=128), in_=qk[:, sl, 0:4, 0:ntok]),
                        r=[r_qk[sl]], dma="qk%d" % sl)
                    T.op("sp", lambda e, sl=sl, t0=t0, ntok=ntok: e.dma_start(
                        out=KT[:, t0 * 128:t0 * 128 + ntok].rearrange("(c p) t -> p c t", p=128), in_=qk[:, sl, 4:8, 0:ntok]),
                        r=[r_qk[sl]], dma="qk%d" % sl)
                    for t in range(ntl):
                        for gi, (c0, c1, o0) in enumerate(((0, 512, 0), (512, 768, 512), (1792, 2304, 768))):
                            pb = (t * 3 + gi) % 2
                            n = c1 - c0
                            for ck in range(8):
                                T.op("pe", lambda e, pb=pb, ck=ck, c0=c0, c1=c1, n=n, sl=sl, t=t: e.matmul(
                                    pt_[:, pb, 0:n], hT[:, sl, ck, t * 128:(t + 1) * 128], weff[:, ck, c0:c1], start=(ck == 0), stop=(ck == 7)),
                                    r=[r_weff, r_hT[sl]], w=[r_pt[pb]])
                            T.op("dve", lambda e, pb=pb, n=n, o0=o0, sl=sl, t=t: e.tensor_copy(tm[:, sl, t, o0:o0 + n], pt_[:, pb, 0:n]),
                                 r=[r_pt[pb]], w=[r_tm[sl]])
                    for (dst, o0, n) in ((UP, 0, 256), (ZC, 256, 256), (ZS, 512, 256), (VV, 768, 512)):
                        T.op("sp", lambda e, dst=dst, o0=o0, n=n, sl=sl, t0=t0, ntl=ntl: e.dma_start(
                            out=dst[t0 * 128:(t0 + ntl) * 128, :].rearrange("(t p) c -> p t c", p=128), in_=tm[:, sl, 0:ntl, o0:o0 + n]),
                            r=[r_tm[sl]], dma="tm%d" % sl)

                stageA1(0)
                for si in range(len(stsA)):
                    if si + 1 < len(stsA):
                        stageA1(si + 1)
                    stageA2(si)
                T.end_phase()

        def phase_pool(l, do_ctx):
            with ExitStack() as es:
                pbm = es.enter_context(nc.sbuf_tensor(U("pbm"), [128, 36, 128], BF16))
                pw = es.enter_context(nc.sbuf_tensor(U("pw"), [64, 4, 64], BF16))
                psc = es.enter_context(nc.sbuf_tensor(U("psc"), [64, 4], F32))
                ut = es.enter_context(nc.sbuf_tensor(U("ut"), [128, 2, 6, 256], BF16))
                dT = es.enter_context(nc.sbuf_tensor(U("dT"), [64, 2, 4, 512], BF16))
                yT = es.enter_context(nc.sbuf_tensor(U("yT"), [64, 2, 4, 512], BF16))
                pd = es.enter_context(nc.psum_tensor(U("pd"), [64, 2, 512], F32))
                py = es.enter_context(nc.psum_tensor(U("py"), [64, 2, 512], F32))
                r_pbm = T.R(); r_pw = T.R(); r_psc = T.R(); r_ut = [T.R(), T.R()]; r_dT = [T.R(), T.R()]; r_yT = [T.R(), T.R()]
                r_pd = [T.R(), T.R()]; r_py = [T.R(), T.R()]
                T.op("sp", lambda e: e.dma_start(out=pbm[:], in_=pb_d.rearrange("g k o p q -> p (g k o) q")), w=[r_pbm], dma="pbm")
                T.op("pool", lambda e: e.dma_start(out=pw[:], in_=pool_w[l].rearrange("g c d -> c g d")), w=[r_pw], dma="pw")
                T.op("sp", lambda e: e.dma_start(out=psc[:], in_=pscol[l]), w=[r_psc], dma="psc")
                sts = super_tiles() if do_ctx else super_tiles()[:16]
                def loadP(si):
                    t0, ntl, k = sts[si]
                    sl = si % 2
                    seq0, seqn = (0, 64) if k == 0 else (64, 2)
                    lo = max(t0 - 1, seq0); hi = min(t0 + ntl + 1, seq0 + seqn)
                    off = lo - (t0 - 1)
                    T.op("sp", lambda e, sl=sl, lo=lo, hi=hi, off=off: e.dma_start(
                        out=ut[:, sl, off:off + hi - lo, :], in_=UP[lo * 128:hi * 128, :].rearrange("(t p) c -> p t c", p=128)),
                        w=[r_ut[sl]], dma="ut%d" % sl)

                loadP(0)
                for si, (t0, ntl, k) in enumerate(sts):
                    sl = si % 2
                    seq0, seqn = (0, 64) if k == 0 else (64, 2)
                    if si + 1 < len(sts):
                        loadP(si + 1)
                    for g in range(4):
                        pb = g % 2
                        for t in range(ntl):
                            tile = t0 + t
                            kind = 0 if tile == seq0 else (2 if tile == seq0 + seqn - 1 else 1)
                            offs = [o for o in (-1, 0, 1) if not (kind == 0 and o == -1) and not (kind == 2 and o == 1)]
                            for oi, o in enumerate(offs):
                                T.op("pe", lambda e, pb=pb, g=g, t=t, o=o, kind=kind, sl=sl, first=(oi == 0), last=(oi == len(offs) - 1): e.matmul(
                                    pd[:, pb, t * 128:(t + 1) * 128], ut[:, sl, t + 1 + o, g * 64:(g + 1) * 64], pbm[:, (g * 3 + kind) * 3 + o + 1, :],
                                    start=first, stop=last), r=[r_ut[sl], r_pbm], w=[r_pd[pb]])
                        T.op("dve", lambda e, pb=pb, g=g, sl=sl, ntl=ntl: e.tensor_copy(dT[:, sl, g, 0:ntl * 128], pd[:, pb, 0:ntl * 128]),
                             r=[r_pd[pb]], w=[r_dT[sl]])
                        T.op("pe", lambda e, pb=pb, g=g, sl=sl, ntl=ntl: e.matmul(py[:, pb, 0:ntl * 128], pw[:, g, :], dT[:, sl, g, 0:ntl * 128], start=True, stop=True),
                             r=[r_pw, r_dT[sl]], w=[r_py[pb]])
                        T.op("act", lambda e, pb=pb, g=g, sl=sl, ntl=ntl: e.activation(out=yT[:, sl, g, 0:ntl * 128], in_=py[:, pb, 0:ntl * 128], func=AF.Identity,
                                                                                   scale=psc[:, g:g + 1]), r=[r_py[pb], r_psc], w=[r_yT[sl]])
                    if k == 1 and os.environ.get("POOLCTX_ALT") == "dst0":
                        T.op("sp", lambda e, sl=sl, t0=t0, ntl=ntl: e.dma_start(
                            out=MIXT[0:256, 0:ntl * 128].rearrange("(g d) t -> d g t", d=64), in_=yT[:, sl, :, 0:ntl * 128]),
                            r=[r_yT[sl]], dma="yT%d" % sl)
                    elif k == 1:
                        for g in range(4):
                            T.op("sp", lambda e, sl=sl, t0=t0, ntl=ntl, g=g: e.dma_start(
                                out=MIXT[g * 64:(g + 1) * 64, t0 * 128:(t0 + ntl) * 128], in_=yT[:, sl, g, 0:ntl * 128]),
                                r=[r_yT[sl]], dma="yT%d" % sl)
                    else:
                        T.op("sp", lambda e, sl=sl, t0=t0, ntl=ntl: e.dma_start(
                            out=MIXT[0:256, t0 * 128:(t0 + ntl) * 128].rearrange("(g d) t -> d g t", d=64), in_=yT[:, sl, :, 0:ntl * 128]),
                            r=[r_yT[sl]], dma="yT%d" % sl)
                T.end_phase()

        def phase_f1():
            with ExitStack() as es:
                z = es.enter_context(nc.sbuf_tensor(U("z"), [128, 128 * 256], BF16))
                ab = es.enter_context(nc.sbuf_tensor(U("ab"), [128, 128 * 256], BF16))
                f1 = es.enter_context(nc.sbuf_tensor(U("f1"), [128, 128], BF16))
                pf = es.enter_context(nc.psum_tensor(U("pf"), [128, 4, 512], F32))
                r_z = T.R(); r_ab = T.R(); r_f1 = T.R(); r_pf = [T.R() for _ in range(4)]
                T.op("sp", lambda e: e.dma_start(out=f1[:], in_=f1_d), w=[r_f1], dma="f1")
                for h_, src in enumerate((ZC, ZS)):
                    for q in range(4):
                        T.op("sp", lambda e, h_=h_, src=src, q=q: e.dma_start(
                            out=z[h_ * 64 + q * 16: h_ * 64 + (q + 1) * 16, :],
                            in_=src[q * 2048:(q + 1) * 2048, :].rearrange("(t s) c -> t (s c)", s=128)), w=[r_z], dma="z")
                for n in range(64):
                    pb = n % 4
                    T.op("pe", lambda e, n=n, pb=pb: e.matmul(pf[:, pb, :], f1[:], z[:, n * 512:(n + 1) * 512], start=True, stop=True),
                         r=[r_f1, r_z], w=[r_pf[pb]])
                    eng = "dve" if n % 2 == 0 else "act"
                    if eng == "dve":
                        T.op("dve", lambda e, n=n, pb=pb: e.tensor_copy(ab[:, n * 512:(n + 1) * 512], pf[:, pb, :]), r=[r_pf[pb]], w=[r_ab])
                    else:
                        T.op("act", lambda e, n=n, pb=pb: e.copy(out=ab[:, n * 512:(n + 1) * 512], in_=pf[:, pb, :]), r=[r_pf[pb]], w=[r_ab])
                for q in range(4):
                    T.op("sp", lambda e, q=q: e.dma_start(out=ABd[:, q * 8192:(q + 1) * 8192], in_=ab[:, q * 8192:(q + 1) * 8192]), r=[r_ab], dma="ab")
                T.end_phase()

        def phase_f2(do_ctx):
            with ExitStack() as es:
                abt = es.enter_context(nc.sbuf_tensor(U("abt"), [128, 128, 256], BF16))
                m2 = es.enter_context(nc.sbuf_tensor(U("m2"), [128, 64, 2, 128], BF16))
                yf = es.enter_context(nc.sbuf_tensor(U("yf"), [128, 2, S], BF16))
                zc = es.enter_context(nc.sbuf_tensor(U("zc"), [128, 2, 2, 256], BF16))
                cd = es.enter_context(nc.sbuf_tensor(U("cd"), [128, 2, 2, 256], BF16))
                yfc = es.enter_context(nc.sbuf_tensor(U("yfc"), [128, 2, 256], BF16))
                pg = es.enter_context(nc.psum_tensor(U("pg"), [128, 2, 4, 128], F32))
                pc = es.enter_context(nc.psum_tensor(U("pc"), [128, 2, 512], F32))
                r_abt = T.R(); r_m2 = T.R(); r_yf = T.R(); r_zc = T.R(); r_cd = T.R(); r_yfc = T.R()
                r_pg = [T.R(), T.R()]; r_pc = [T.R(), T.R()]
                T.op("sp", lambda e: e.dma_start(out=m2[:], in_=m2_d.rearrange("p (k a j) -> p k a j", k=64, a=2)), w=[r_m2], dma="m2")
                abv = ABd.rearrange("q (s c) -> s q c", c=256)
                for q in range(8):
                    T.op("sp", lambda e, q=q: e.dma_start(out=abt[:, q * 16:(q + 1) * 16, :], in_=abv[:, q * 16:(q + 1) * 16, :]), w=[r_abt], dma="abt")
                yv = yf[:].rearrange("p c (k1 k2) -> p c k2 k1", k2=64)
                it = 0
                for c2 in range(2):
                    for kq in range(16):
                        pb = it % 2
                        it += 1
                        for ki in range(4):
                            k2 = kq * 4 + ki
                            T.op("pe", lambda e, pb=pb, ki=ki, k2=k2, c2=c2: e.matmul(pg[:, pb, ki, :], abt[:, k2, c2 * 128:(c2 + 1) * 128], m2[:, k2, 0, :],
                                                                                start=True, stop=False), r=[r_abt, r_m2], w=[r_pg[pb]])
                            T.op("pe", lambda e, pb=pb, ki=ki, k2=k2, c2=c2: e.matmul(pg[:, pb, ki, :], abt[:, 64 + k2, c2 * 128:(c2 + 1) * 128], m2[:, k2, 1, :],
                                                                                start=False, stop=True), r=[r_abt, r_m2], w=[r_pg[pb]])
                        if it % 2 == 0:
                            T.op("dve", lambda e, pb=pb, kq=kq, c2=c2: e.tensor_copy(yv[:, c2, kq * 4:(kq + 1) * 4, :], pg[:, pb, :, :]), r=[r_pg[pb]], w=[r_yf])
                        else:
                            T.op("act", lambda e, pb=pb, kq=kq, c2=c2: e.copy(out=yv[:, c2, kq * 4:(kq + 1) * 4, :], in_=pg[:, pb, :, :]), r=[r_pg[pb]], w=[r_yf])
                for c2 in range(2):
                    T.op("sp", lambda e, c2=c2: e.dma_start(out=MIXT[256 + c2 * 128: 256 + (c2 + 1) * 128, 0:S], in_=yf[:, c2, :]), r=[r_yf], dma="yf")
                if do_ctx:
                    T.op("sp", lambda e: e.dma_start(out=cd[:], in_=cd_d.rearrange("p (t a k) -> p t a k", t=2, a=2)), w=[r_cd], dma="cd")
                    for a, src in enumerate((ZC, ZS)):
                        T.op("sp", lambda e, a=a, src=src: e.dma_start(out=zc[:, a, :, :], in_=src[S:S + CT, :].rearrange("(t p) c -> p t c", p=128)),
                             w=[r_zc], dma="zc")
                    for c2 in range(2):
                        i = 0
                        for a in range(2):
                            for t in range(2):
                                T.op("pe", lambda e, c2=c2, a=a, t=t, i=i: e.matmul(pc[:, c2, 0:256], zc[:, a, t, c2 * 128:(c2 + 1) * 128], cd[:, t, a, :],
                                                                              start=(i == 0), stop=(i == 3)), r=[r_zc, r_cd], w=[r_pc[c2]])
                                i += 1
                        T.op("dve", lambda e, c2=c2: e.tensor_copy(yfc[:, c2, :], pc[:, c2, 0:256]), r=[r_pc[c2]], w=[r_yfc])
                    for c2 in range(2):
                        T.op("sp", lambda e, c2=c2: e.dma_start(out=MIXT[256 + c2 * 128:256 + (c2 + 1) * 128, S:S + CT], in_=yfc[:, c2, :]), r=[r_yfc], dma="yfc")
                T.end_phase()

        def phase_attn(l, do_ctx):
            with ExitStack() as es:
                bmt = es.enter_context(nc.sbuf_tensor(U("bmt"), [128, 5, 8, 640], BF16))
                idb = es.enter_context(nc.sbuf_tensor(U("idb2"), [128, 128], BF16))
                ktc = es.enter_context(nc.sbuf_tensor(U("ktc"), [128, 4, 256], BF16))
                vc = es.enter_context(nc.sbuf_tensor(U("vc"), [128, 2, 512], BF16))
                qt = es.enter_context(nc.sbuf_tensor(U("qt"), [128, 2, 4, 128], BF16))
                ktw = es.enter_context(nc.sbuf_tensor(U("ktw"), [128, 2, 4, 640], BF16))
                vw = es.enter_context(nc.sbuf_tensor(U("vw"), [128, 2, 5, 512], BF16))
                pm = es.enter_context(nc.sbuf_tensor(U("pm"), [128, 2, 1024], BF16))
                ptT = es.enter_context(nc.sbuf_tensor(U("ptT"), [128, 2, 7, 128], BF16))
                stt = es.enter_context(nc.sbuf_tensor(U("st"), [128, 2, 4], F32))
                osb = es.enter_context(nc.sbuf_tensor(U("osb"), [128, 2, 512], BF16))
                oT = es.enter_context(nc.sbuf_tensor(U("oT"), [128, 2, 4, 512], BF16))
                psS = es.enter_context(nc.psum_tensor(U("psS"), [128, 2, 2, 512], F32))
                psT = es.enter_context(nc.psum_tensor(U("psT"), [128, 2, 8, 128], BF16))
                psO = es.enter_context(nc.psum_tensor(U("psO"), [128, 512], F32))
                psX = es.enter_context(nc.psum_tensor(U("psX"), [128, 4, 128], BF16))
                r_bmt = T.R(); r_idb = T.R(); r_ktc = T.R(); r_vc = T.R(); r_qt = [T.R(), T.R()]; r_ktw = [T.R(), T.R()]; r_vw = [T.R(), T.R()]
                r_pm = [T.R(), T.R()]; r_ptT = [T.R(), T.R()]; r_st = [T.R(), T.R()]; r_osb = [T.R(), T.R()]; r_oT = [T.R(), T.R()]
                r_S = [T.R(), T.R()]; r_T = [T.R(), T.R()]; r_O = T.R(); r_X = T.R()
                T.op("sp", lambda e: e.dma_start(out=idb[:], in_=identb_d), w=[r_idb], dma="idb")
                for cs_ in range(5):
                    T.op("sp", lambda e, cs_=cs_: e.dma_start(out=bmt[:, cs_, :, :], in_=bm_d[l, cs_].rearrange("p (h k) -> p h k", h=8)), w=[r_bmt], dma="bmt")
                T.op("sp", lambda e: e.dma_start(out=ktc[:], in_=KT[:, S:S + CT].rearrange("(c p) t -> p c t", p=128)), w=[r_ktc], dma="ktc")
                T.op("sp", lambda e: e.dma_start(out=vc[:], in_=VV[S:S + CT, :].rearrange("(t p) c -> p t c", p=128)), w=[r_vc], dma="vc")
                for sb in range(2):
                    T.op("dve", lambda e, sb=sb: e.memset(psS[:, sb, 1, 384:512], NEG), w=[r_S[sb]])
                units = [(j, 0) for j in range(64)] + ([(64, 1), (65, 1)] if do_ctx else [])
                if os.environ.get("ATTN_UNITS"):
                    units = [u for u in units if u[0] in [int(v) for v in os.environ["ATTN_UNITS"].split(",")]]
                flat = [(ui, j, isctx, h) for ui, (j, isctx) in enumerate(units) for h in range(8)]

                def keyinfo(ui, isctx, sb):
                    sl = ui % 2
                    if not isctx:
                        return (psS[:, sb, :, :].rearrange("p a k -> p (a k)")[:, 0:896], pm[:, sb, 0:896],
                                [(0, vw, sl, b) for b in range(5)] + [(1, vc, None, b) for b in range(2)])
                    return (psS[:, sb, 0, 0:256], pm[:, sb, 0:256], [(1, vc, None, b) for b in range(2)])

                def stage1(n):
                    ui, j, isctx, h = flat[n]
                    sl = ui % 2
                    sb = n % 2
                    def loads(ui2):
                        j2, isctx2 = units[ui2]
                        sl2 = ui2 % 2
                        T.op("sp", lambda e, sl2=sl2, j2=j2: e.dma_start(out=qt[:, sl2, :, :], in_=QT[:, j2 * 128:(j2 + 1) * 128].rearrange("(c p) t -> p c t", p=128)),
                             w=[r_qt[sl2]], dma="qt%d" % sl2)
                        if not isctx2:
                            start2 = min(max(2 * j2 - 4, 0), 118)
                            T.op("sp", lambda e, sl2=sl2, start2=start2: e.dma_start(
                                out=ktw[:, sl2, :, :], in_=KT[:, start2 * 64: start2 * 64 + 640].rearrange("(c p) t -> p c t", p=128)), w=[r_ktw[sl2]], dma="ktw%d" % sl2)
                            T.op("sp", lambda e, sl2=sl2, start2=start2: e.dma_start(
                                out=vw[:, sl2, :, :], in_=VV[start2 * 64: start2 * 64 + 640, :].rearrange("(t p) c -> p t c", p=128)), w=[r_vw[sl2]], dma="vw%d" % sl2)

                    if n == 0:
                        loads(0)
                    if h == 1 and ui + 1 < len(units):
                        loads(ui + 1)
                    c = h // 2
                    p0 = (h % 2) * 64
                    qv = qt[p0:p0 + 64, sl, c, :]
                    if not isctx:
                        start = min(max(2 * j - 4, 0), 118)
                        case = (2 * j - start) // 2
                        T.op("pe", lambda e, sb=sb, qv=qv, p0=p0, c=c, sl=sl: e.matmul(psS[:, sb, 0, :], qv, ktw[p0:p0 + 64, sl, c, 0:512], start=True, stop=False),
                             r=[r_qt[sl], r_ktw[sl]], w=[r_S[sb]])
                        T.op("pe", lambda e, sb=sb, h=h, case=case: e.matmul(psS[:, sb, 0, :], idb[:], bmt[:, case, h, 0:512], start=False, stop=True),
                             r=[r_idb, r_bmt], w=[r_S[sb]])
                        T.op("pe", lambda e, sb=sb, qv=qv, p0=p0, c=c, sl=sl: e.matmul(psS[:, sb, 1, 0:128], qv, ktw[p0:p0 + 64, sl, c, 512:640], start=True, stop=False),
                             r=[r_qt[sl], r_ktw[sl]], w=[r_S[sb]])
                        T.op("pe", lambda e, sb=sb, h=h, case=case: e.matmul(psS[:, sb, 1, 0:128], idb[:], bmt[:, case, h, 512:640], start=False, stop=True),
                             r=[r_idb, r_bmt], w=[r_S[sb]])
                        T.op("pe", lambda e, sb=sb, qv=qv, p0=p0, c=c: e.matmul(psS[:, sb, 1, 128:384], qv, ktc[p0:p0 + 64, c, :], start=True, stop=True),
                             r=[r_qt[sl], r_ktc], w=[r_S[sb]])
                    else:
                        T.op("pe", lambda e, sb=sb, qv=qv, p0=p0, c=c: e.matmul(psS[:, sb, 0, 0:256], qv, ktc[p0:p0 + 64, c, :], start=True, stop=True),
                             r=[r_qt[sl], r_ktc], w=[r_S[sb]])
                    sview, pview, _ = keyinfo(ui, isctx, sb)
                    T.op("dve", lambda e, sb=sb, sview=sview: e.tensor_reduce(out=stt[:, sb, 0:1], in_=sview, axis=AX.X, op=ALU.max),
                         r=[r_S[sb]], w=[r_st[sb]])
                    T.op("dve", lambda e, sb=sb: e.tensor_scalar(out=stt[:, sb, 1:2], in0=stt[:, sb, 0:1], scalar1=-1.0, scalar2=None, op0=ALU.mult),
                         r=[r_st[sb]], w=[r_st[sb]])
                    T.op("act", lambda e, sb=sb, sview=sview, pview=pview: e.activation(out=pview, in_=sview, func=AF.Exp, bias=stt[:, sb, 1:2],
                                                                                  accum_out=stt[:, sb, 2:3]), r=[r_S[sb], r_st[sb]], w=[r_pm[sb], r_st[sb]])

                def stage2(n):
                    ui, j, isctx, h = flat[n]
                    sl = ui % 2
                    sb = n % 2
                    _, _, blocks = keyinfo(ui, isctx, sb)
                    nb = len(blocks)
                    for bi in range(nb):
                        T.op("pe", lambda e, sb=sb, bi=bi: e.transpose(psT[:, sb, bi, :], pm[:, sb, bi * 128:(bi + 1) * 128], idb[:]),
                             r=[r_pm[sb], r_idb], w=[r_T[sb]])
                    T.op("dve", lambda e, sb=sb, nb=nb: e.tensor_copy(ptT[:, sb, 0:nb, :], psT[:, sb, 0:nb, :]), r=[r_T[sb]], w=[r_ptT[sb]])
                    for bi, (isc, vbuf, vsl, b) in enumerate(blocks):
                        rv = vbuf[:, b, h * 64:(h + 1) * 64] if isc else vbuf[:, vsl, b, h * 64:(h + 1) * 64]
                        T.op("pe", lambda e, sb=sb, bi=bi, rv=rv, h=h, nb=nb: e.matmul(psO[:, h * 64:(h + 1) * 64], ptT[:, sb, bi, :], rv,
                                                                                 start=(bi == 0), stop=(bi == nb - 1)),
                             r=[r_ptT[sb], r_vc] + ([r_vw[sl]] if not isctx else []), w=[r_O])
                    T.op("dve", lambda e, sb=sb: e.reciprocal(out=stt[:, sb, 3:4], in_=stt[:, sb, 2:3]), r=[r_st[sb]], w=[r_st[sb]])
                    T.op("act", lambda e, sb=sb, sl=sl, h=h: e.activation(out=osb[:, sl, h * 64:(h + 1) * 64], in_=psO[:, h * 64:(h + 1) * 64],
                                                                        func=AF.Identity, scale=stt[:, sb, 3:4]), r=[r_O, r_st[sb]], w=[r_osb[sl]])
                    if h != 7:
                        return
                    if isctx:
                        grp, gi, gn_ = 16, j - 64, 2
                    else:
                        grp, gi, gn_ = j // 4, j % 4, 4
                    osl = grp % 2
                    for c in range(4):
                        T.op("pe", lambda e, sl=sl, c=c: e.transpose(psX[:, c, :], osb[:, sl, c * 128:(c + 1) * 128], idb[:]), r=[r_osb[sl], r_idb], w=[r_X])
                    T.op("dve", lambda e, osl=osl, gi=gi: e.tensor_copy(oT[:, osl, :, gi * 128:(gi + 1) * 128], psX[:, :, :]), r=[r_X], w=[r_oT[osl]])
                    if gi == gn_ - 1:
                        tok0 = (grp * 4 * 128) if not isctx else S
                        if isctx:
                            for c in range(4):
                                T.op("sp", lambda e, osl=osl, tok0=tok0, gn_=gn_, c=c: e.dma_start(
                                    out=MIXT[512 + c * 128:512 + (c + 1) * 128, tok0:tok0 + gn_ * 128], in_=oT[:, osl, c, 0:gn_ * 128]),
                                    r=[r_oT[osl]], dma="oT%d" % osl)
                        else:
                            T.op("sp", lambda e, osl=osl, tok0=tok0, gn_=gn_: e.dma_start(
                                out=MIXT[512:1024, tok0:tok0 + gn_ * 128].rearrange("(c p) t -> p c t", p=128), in_=oT[:, osl, :, 0:gn_ * 128]),
                                r=[r_oT[osl]], dma="oT%d" % osl)

                stage1(0)
                for n in range(len(flat)):
                    if n + 1 < len(flat):
                        stage1(n + 1)
                    stage2(n)
                T.end_phase()

        def phase_C(l, do_ctx):
            with ExitStack() as es:
                wo = es.enter_context(nc.sbuf_tensor(U("wo"), [128, 8, D], BF16))
                g1 = es.enter_context(nc.sbuf_tensor(U("g1"), [128, 2, D], F32))
                mx = es.enter_context(nc.sbuf_tensor(U("mx"), [128, 2, 8, 512], BF16))
                xc = es.enter_context(nc.sbuf_tensor(U("xc"), [128, 2, 4, D], F32))
                tmp = es.enter_context(nc.sbuf_tensor(U("tmp"), [128, 2, 512], F32))
                po = es.enter_context(nc.psum_tensor(U("po"), [128, 4, 512], F32))
                r_wo = T.R(); r_g1 = T.R(); r_mx = [T.R(), T.R()]; r_xc = [T.R(), T.R()]; r_tmp = [T.R(), T.R()]; r_po = [T.R() for _ in range(4)]
                for ck in range(8):
                    T.op("pool", lambda e, ck=ck: e.dma_start(out=wo[:, ck, :], in_=w_out[l, ck * 128:(ck + 1) * 128, :]), w=[r_wo], dma="wo")
                for k in range(2):
                    T.op("sp", lambda e, k=k: e.dma_start(out=g1[:, k, :], in_=MOD[l, k][:, 2 * D:3 * D]), w=[r_g1], dma="g1")
                sts = super_tiles() if do_ctx else super_tiles()[:16]
                it = 0
                def loadC(si):
                    t0, ntl, k = sts[si]
                    sl = si % 2
                    ntok = ntl * 128
                    T.op("sp", lambda e, sl=sl, t0=t0, ntok=ntok: e.dma_start(
                        out=mx[:, sl, :, 0:ntok], in_=MIXT[0:D, t0 * 128:t0 * 128 + ntok].rearrange("(c p) t -> p c t", p=128)), w=[r_mx[sl]], dma="mx%d" % sl)
                    T.op("sp", lambda e, sl=sl, t0=t0, ntl=ntl: e.dma_start(
                        out=xc[:, sl, 0:ntl, :], in_=XS[t0 * 128:(t0 + ntl) * 128, :].rearrange("(t p) d -> p t d", p=128)), w=[r_xc[sl]], dma="xc%d" % sl)

                loadC(0)
                for si, (t0, ntl, k) in enumerate(sts):
                    sl = si % 2
                    ntok = ntl * 128
                    if si + 1 < len(sts):
                        loadC(si + 1)
                    for t in range(ntl):
                        for hf in range(2):
                            pb = it % 4
                            tb = it % 2
                            it += 1
                            for ck in range(8):
                                T.op("pe", lambda e, pb=pb, ck=ck, sl=sl, t=t, hf=hf: e.matmul(po[:, pb, :], mx[:, sl, ck, t * 128:(t + 1) * 128],
                                                                                         wo[:, ck, hf * 512:(hf + 1) * 512], start=(ck == 0), stop=(ck == 7)),
                                     r=[r_mx[sl], r_wo], w=[r_po[pb]])
                            T.op("dve", lambda e, pb=pb, tb=tb, k=k, hf=hf: e.tensor_tensor(out=tmp[:, tb, :], in0=po[:, pb, :], in1=g1[:, k, hf * 512:(hf + 1) * 512], op=ALU.mult),
                                 r=[r_po[pb], r_g1], w=[r_tmp[tb]])
                            T.op("pool", lambda e, tb=tb, sl=sl, t=t, hf=hf: e.tensor_tensor(out=xc[:, sl, t, hf * 512:(hf + 1) * 512], in0=xc[:, sl, t, hf * 512:(hf + 1) * 512],
                                                                                        in1=tmp[:, tb, :], op=ALU.add), r=[r_tmp[tb], r_xc[sl]], w=[r_xc[sl]])
                    T.op("sp", lambda e, sl=sl, t0=t0, ntl=ntl: e.dma_start(
                        out=XS[t0 * 128:(t0 + ntl) * 128, :].rearrange("(t p) d -> p t d", p=128), in_=xc[:, sl, 0:ntl, :]), r=[r_xc[sl]], dma="xc%d" % sl)
                T.end_phase()

        def phase_D(l, do_ctx):
            with ExitStack() as es:
                w1s = es.enter_context(nc.sbuf_tensor(U("w1s"), [128, 8, DFF], BF16))
                w3s = es.enter_context(nc.sbuf_tensor(U("w3s"), [128, 8, DFF], BF16))
                w2s = es.enter_context(nc.sbuf_tensor(U("w2s"), [128, NF, D], BF16))
                idb = es.enter_context(nc.sbuf_tensor(U("idb3"), [128, 128], BF16))
                md = es.enter_context(nc.sbuf_tensor(U("md"), [128, 3, D], F32))
                xd = es.enter_context(nc.sbuf_tensor(U("xd"), [128, 4, D], F32))
                ssd = es.enter_context(nc.sbuf_tensor(U("ssd"), [128, 4], F32))
                hbd = es.enter_context(nc.sbuf_tensor(U("hbd"), [128, 2, D], BF16))
                hTd = es.enter_context(nc.sbuf_tensor(U("hTd"), [128, 8, 512], BF16))
                sg = es.enter_context(nc.sbuf_tensor(U("sg"), [128, 2, 512], F32))
                gT = es.enter_context(nc.sbuf_tensor(U("gT"), [128, NF, 512], BF16))
                tmpd = es.enter_context(nc.sbuf_tensor(U("tmpd"), [128, 2, 512], F32))
                ptd = es.enter_context(nc.psum_tensor(U("ptd"), [128, 8, 128], BF16))
                pa = es.enter_context(nc.psum_tensor(U("pa"), [128, 2, 2, 512], F32))
                pod = es.enter_context(nc.psum_tensor(U("pod"), [128, 2, 512], F32))
                r_w1 = T.R(); r_w3 = T.R(); r_w2 = T.R(); r_idb = T.R(); r_md = T.R(); r_xd = T.R(); r_ss = T.R()
                r_hb = [T.R(), T.R()]; r_hT = T.R(); r_sg = [T.R(), T.R()]; r_gT = T.R(); r_tmp = [T.R(), T.R()]
                hf_ = tmpd[:].rearrange("p a n -> p (a n)")
                r_ptd = T.R(); r_pa = [T.R(), T.R()]; r_pod = [T.R(), T.R()]
                T.op("sp", lambda e: e.dma_start(out=idb[:], in_=identb_d), w=[r_idb], dma="idb")
                for ck in range(8):
                    T.op("pool", lambda e, ck=ck: e.dma_start(out=w1s[:, ck, :], in_=w1[l, ck * 128:(ck + 1) * 128, :]), w=[r_w1], dma="w1")
                    T.op("pool", lambda e, ck=ck: e.dma_start(out=w3s[:, ck, :], in_=w3[l, ck * 128:(ck + 1) * 128, :]), w=[r_w3], dma="w3")
                for f in range(NF):
                    T.op("pool", lambda e, f=f: e.dma_start(out=w2s[:, f, :], in_=w2[l, f * 128:(f + 1) * 128, :]), w=[r_w2], dma="w2")
                sts = super_tiles() if do_ctx else super_tiles()[:16]
                it = 0
                for si, (t0, ntl, k) in enumerate(sts):
                    ntok = ntl * 128
                    if si == 0 or k != sts[si - 1][2]:
                        T.op("sp", lambda e, k=k: e.dma_start(out=md[:], in_=MOD[l, k][:, 3 * D:6 * D].rearrange("p (a d) -> p a d", a=3)), w=[r_md], dma="md")
                    T.op("sp", lambda e, t0=t0, ntl=ntl: e.dma_start(out=xd[:, 0:ntl, :], in_=XS[t0 * 128:(t0 + ntl) * 128, :].rearrange("(t p) d -> p t d", p=128)),
                         w=[r_xd], dma="xd")
                    for t in range(ntl):
                        T.op("act", lambda e, t=t: e.activation(out=hf_, in_=xd[:, t, :], func=AF.Square, accum_out=ssd[:, t:t + 1]), r=[r_xd], w=[r_tmp[0], r_tmp[1], r_ss])
                    T.op("dve", lambda e: e.tensor_scalar(out=ssd[:], in0=ssd[:], scalar1=1.0 / D, scalar2=EPS, op0=ALU.mult, op1=ALU.add), r=[r_ss], w=[r_ss])
                    T.op("act", lambda e: e.activation(out=ssd[:], in_=ssd[:], func=AF.Sqrt), r=[r_ss], w=[r_ss])
                    T.op("dve", lambda e: e.reciprocal(out=ssd[:], in_=ssd[:]), r=[r_ss], w=[r_ss])
                    for t in range(ntl):
                        hs = t % 2
                        T.op("dve", lambda e, t=t, k=k: e.scalar_tensor_tensor(out=hf_, in0=xd[:, t, :], scalar=ssd[:, t:t + 1], in1=md[:, 1, :],
                                                                              op0=ALU.mult, op1=ALU.mult), r=[r_xd, r_ss, r_md], w=[r_tmp[0], r_tmp[1]])
                        T.op("dve", lambda e, hs=hs, k=k: e.tensor_tensor(out=hbd[:, hs, :], in0=hf_, in1=md[:, 0, :], op=ALU.add), r=[r_tmp[0], r_tmp[1], r_md], w=[r_hb[hs]])
                        for ck in range(8):
                            T.op("pe", lambda e, hs=hs, ck=ck: e.transpose(ptd[:, ck, :], hbd[:, hs, ck * 128:(ck + 1) * 128], idb[:]), r=[r_hb[hs], r_idb], w=[r_ptd])
                        T.op("act", lambda e, t=t: e.copy(out=hTd[:, :, t * 128:(t + 1) * 128], in_=ptd[:, :, :]), r=[r_ptd], w=[r_hT])
                    for f in range(NF):
                        pb = f % 2
                        for wi, (ws, rw) in enumerate(((w1s, r_w1), (w3s, r_w3))):
                            for ck in range(8):
                                T.op("pe", lambda e, pb=pb, wi=wi, ws=ws, ck=ck, f=f, ntok=ntok: e.matmul(pa[:, pb, wi, 0:ntok], ws[:, ck, f * 128:(f + 1) * 128], hTd[:, ck, 0:ntok],
                                                                                                    start=(ck == 0), stop=(ck == 7)), r=[rw, r_hT], w=[r_pa[pb]])
                        T.op("act", lambda e, pb=pb, ntok=ntok: e.activation(out=sg[:, pb, 0:ntok], in_=pa[:, pb, 0, 0:ntok], func=AF.Silu), r=[r_pa[pb]], w=[r_sg[pb]])
                        T.op("dve", lambda e, pb=pb, f=f, ntok=ntok: e.tensor_tensor(out=gT[:, f, 0:ntok], in0=sg[:, pb, 0:ntok], in1=pa[:, pb, 1, 0:ntok], op=ALU.mult),
                             r=[r_sg[pb], r_pa[pb]], w=[r_gT])
                    for t in range(ntl):
                        for hfi in range(2):
                            pb = it % 2
                            it += 1
                            for f in range(NF):
                                T.op("pe", lambda e, pb=pb, f=f, t=t, hfi=hfi: e.matmul(pod[:, pb, :], gT[:, f, t * 128:(t + 1) * 128], w2s[:, f, hfi * 512:(hfi + 1) * 512],
                                                                                   start=(f == 0), stop=(f == NF - 1)), r=[r_gT, r_w2], w=[r_pod[pb]])
                            T.op("dve", lambda e, pb=pb, k=k, hfi=hfi: e.tensor_tensor(out=tmpd[:, pb, :], in0=pod[:, pb, :], in1=md[:, 2, hfi * 512:(hfi + 1) * 512], op=ALU.mult),
                                 r=[r_pod[pb], r_md], w=[r_tmp[pb]])
                            T.op("pool", lambda e, pb=pb, t=t, hfi=hfi: e.tensor_tensor(out=xd[:, t, hfi * 512:(hfi + 1) * 512], in0=xd[:, t, hfi * 512:(hfi + 1) * 512],
                                                                                   in1=tmpd[:, pb, :], op=ALU.add), r=[r_tmp[pb], r_xd], w=[r_xd])
                    T.op("sp", lambda e, t0=t0, ntl=ntl: e.dma_start(out=XS[t0 * 128:(t0 + ntl) * 128, :].rearrange("(t p) d -> p t d", p=128), in_=xd[:, 0:ntl, :]),
                         r=[r_xd], dma="xd")
                T.end_phase()

        def phase_final():
            with ExitStack() as es:
                fg = es.enter_context(nc.sbuf_tensor(U("fg"), [128, D], F32))
                xf = es.enter_context(nc.sbuf_tensor(U("xf"), [128, 2, 4, D], F32))
                jk = es.enter_context(nc.sbuf_tensor(U("jk"), [128, D], F32))
                sf = es.enter_context(nc.sbuf_tensor(U("sf"), [128, 8], F32))
                r_fg = T.R(); r_xf = [T.R(), T.R()]; r_jk = T.R(); r_sf = [T.R(), T.R()]
                T.op("sp", lambda e: e.dma_start(out=fg[:], in_=bcast_rows(fing)), w=[r_fg], dma="fg")
                for si in range(16):
                    sl = si % 2
                    T.op("sp", lambda e, sl=sl, si=si: e.dma_start(out=xf[:, sl, :, :], in_=XS[si * 512:(si + 1) * 512, :].rearrange("(t p) d -> p t d", p=128)),
                         w=[r_xf[sl]], dma="xf%d" % sl)
                    for t in range(4):
                        T.op("act", lambda e, sl=sl, t=t: e.activation(out=jk[:], in_=xf[:, sl, t, :], func=AF.Square, accum_out=sf[:, sl * 4 + t: sl * 4 + t + 1]),
                             r=[r_xf[sl]], w=[r_jk, r_sf[sl]])
                    T.op("dve", lambda e, sl=sl: e.tensor_scalar(out=sf[:, sl * 4:sl * 4 + 4], in0=sf[:, sl * 4:sl * 4 + 4], scalar1=1.0 / D, scalar2=EPS,
                                                                 op0=ALU.mult, op1=ALU.add), r=[r_sf[sl]], w=[r_sf[sl]])
                    T.op("act", lambda e, sl=sl: e.activation(out=sf[:, sl * 4:sl * 4 + 4], in_=sf[:, sl * 4:sl * 4 + 4], func=AF.Sqrt), r=[r_sf[sl]], w=[r_sf[sl]])
                    T.op("dve", lambda e, sl=sl: e.reciprocal(out=sf[:, sl * 4:sl * 4 + 4], in_=sf[:, sl * 4:sl * 4 + 4]), r=[r_sf[sl]], w=[r_sf[sl]])
                    for t in range(4):
                        T.op("dve", lambda e, sl=sl, t=t: e.scalar_tensor_tensor(
                            out=xf[:, sl, t, :], in0=xf[:, sl, t, :], scalar=sf[:, sl * 4 + t: sl * 4 + t + 1], in1=fg[:], op0=ALU.mult, op1=ALU.mult),
                            r=[r_sf[sl], r_fg, r_xf[sl]], w=[r_xf[sl]])
                    T.op("sp", lambda e, sl=sl, si=si: e.dma_start(out=out[si * 512:(si + 1) * 512, :].rearrange("(t p) d -> p t d", p=128), in_=xf[:, sl, :, :]),
                         r=[r_xf[sl]], dma="xf%d" % sl)
                T.end_phase()

        on = (lambda n: only is None or n in only)
        if on("pro"):
            phase_prologue()
        for l in range(nlayers):
            do_ctx = (l < nlayers - 1) or force_ctx
            cm = (lambda n: do_ctx and (ctx_mask is None or n in ctx_mask))
            if on("A"):
                phase_A(l)
            if on("pool"):
                phase_pool(l, cm("pool"))
            if on("f1"):
                phase_f1()
            if on("f2"):
                phase_f2(cm("f2"))
            if on("attn"):
                phase_attn(l, cm("attn"))
            if on("C"):
                phase_C(l, cm("C"))
            if on("D"):
                phase_D(l, cm("D"))
        if on("fin"):
            phase_final()
        print('icount', T.icount, 'ninst', nc.n_instructions if not callable(nc.n_instructions) else nc.n_instructions()); print('tracker counts', T.ecount, {k: v for k, v in T.dcount.items() if v > 50}, 'nsem', len(T.dsem) + 5, 'total ops', T.total)
    return nc


def _bf(a):
    return np.ascontiguousarray(a.astype(ml_dtypes.bfloat16))


def host_constants(nat_bias, fourier_w, pool_scale):
    c = {}
    c["identb"] = _bf(np.eye(128, dtype=np.float32))
    a = np.arange(64)
    ang = 2 * np.pi * np.outer(a, a) / 64.0
    C64 = np.cos(ang); S64 = np.sin(ang)
    ccbd = np.zeros((2, 128, 128), np.float64)
    for i, m in enumerate((C64, S64)):
        ccbd[i, :64, :64] = m.T
        ccbd[i, 64:, 64:] = m.T
    c["ccbd"] = _bf(ccbd)
    pb = np.zeros((4, 3, 3, 128, 128), np.float64)
    Sp = 384
    t = np.arange(Sp)
    for g, w in enumerate((2, 4, 8, 16)):
        lo = np.clip(t - w // 2, 0, Sp); hi = np.clip(t + w - w // 2, 0, Sp)
        B = np.zeros((Sp, Sp))
        for tt in range(Sp):
            B[tt, lo[tt]:hi[tt]] = 1.0 / (hi[tt] - lo[tt])
            B[tt, tt] -= 1.0
        for kind in range(3):
            for o in (-1, 0, 1):
                st = kind + o
                if 0 <= st < 3:
                    pb[g, kind, o + 1] = B[kind * 128:(kind + 1) * 128, st * 128:(st + 1) * 128].T
    c["pb"] = _bf(pb)
    nrm = 1.0 / np.sqrt(S * 64.0)
    ph = 2 * np.pi * np.outer(a, a) / 64.0
    f1 = np.zeros((128, 128))
    f1[:64, :64] = np.cos(ph) * nrm; f1[64:, :64] = -np.sin(ph) * nrm
    f1[:64, 64:] = np.sin(ph) * nrm; f1[64:, 64:] = np.cos(ph) * nrm
    c["f1"] = _bf(f1)
    s1 = np.arange(128)[:, None, None]; k2 = np.arange(64)[None, :, None]; k1 = np.arange(128)[None, None, :]
    psi = 2 * np.pi * ((s1 * (64 * k1 + k2)) % S) / float(S)
    m2 = np.stack([np.cos(psi), -np.sin(psi)], axis=2)
    c["m2"] = _bf(m2.reshape(128, -1))
    nrc = 1.0 / np.sqrt(CT * 64.0)
    p = np.arange(128)[:, None, None]; tt = np.arange(2)[None, :, None]; kk = np.arange(CT)[None, None, :]
    th = 2 * np.pi * (((tt * 128 + p) * kk) % CT) / float(CT)
    cd = np.stack([np.cos(th) * nrc, -np.sin(th) * nrc], axis=2)
    c["cd"] = _bf(cd.reshape(128, -1))
    bm = np.full((L, 5, 8, 128, 640), NEG, np.float32)
    q = np.arange(128); qr = q // 64; qc = q % 64
    key = np.arange(640); ki = key // 64; kc = key % 64
    cs = np.clip(qc - 8, 0, 48)
    for case, j in enumerate((0, 1, 2, 62, 63)):
        start = min(max(2 * j - 4, 0), 118)
        r = 2 * j + qr
        rs = np.clip(r - 4, 0, 120)
        krow = start + ki
        ok = (krow[None, :] >= rs[:, None]) & (krow[None, :] < rs[:, None] + 8) & \
             (kc[None, :] >= cs[:, None]) & (kc[None, :] < cs[:, None] + 16)
        dr = np.clip(krow[None, :] - r[:, None] + 7, 0, 14)
        dc = np.clip(kc[None, :] - qc[:, None] + 15, 0, 30)
        for l in range(L):
            g = nat_bias[l][:, dr, dc]
            bm[l, case] = np.where(ok[None], g, NEG)
    c["bm"] = _bf(np.ascontiguousarray(bm.transpose(0, 1, 3, 2, 4)).reshape(L, 5, 128, 8 * 640))
    fwbd = np.zeros((L, 2, 128, 128), np.float32)
    for l in range(L):
        for c2 in range(2):
            fwbd[l, c2, :64, :64] = fourier_w[l, 2 * c2]
            fwbd[l, c2, 64:, 64:] = fourier_w[l, 2 * c2 + 1]
    c["fwbd"] = fwbd
    c["pscol"] = np.ascontiguousarray(pool_scale.reshape(L, 4, 64).transpose(0, 2, 1))
    return c


_NC_CACHE = {}


def kernel(x, c, ctx, c_ctx, w_ada, b_ada, norm1_g, w_in, pool_w, pool_scale, fourier_w,
           nat_bias, w_out, norm2_g, w_ffn1, w_ffn3, w_ffn2, final_g):
    f = lambda a: np.ascontiguousarray(np.asarray(a, dtype=np.float32))
    x = f(x); c = f(c); ctx = f(ctx); c_ctx = f(c_ctx)
    consts = host_constants(f(nat_bias), f(fourier_w), f(pool_scale))
    shared = {
        "w_ada": f(w_ada), "b_ada": f(b_ada), "norm1_g": f(norm1_g), "norm2_g": f(norm2_g), "w_in": f(w_in),
        "pool_w": f(pool_w), "w_out": f(w_out), "w_ffn1": f(w_ffn1), "w_ffn3": f(w_ffn3), "w_ffn2": f(w_ffn2),
        "final_g": f(final_g).reshape(1, D),
    }
    shared.update(consts)
    if "nc" not in _NC_CACHE:
        _NC_CACHE["nc"] = build()
    nc = _NC_CACHE["nc"]
    in_maps = []
    for core in range(NCORES):
        b = core % 4
        ccm = np.concatenate([c[b].reshape(8, 128).T, c_ctx.reshape(8, 128).T], axis=1)
        m = dict(shared)
        m["x"] = x[b]
        m["ctx"] = ctx[b]
        m["cc"] = np.ascontiguousarray(ccm)
        in_maps.append(m)
    res = run_bass_kernel_spmd(nc, in_maps, core_ids=list(range(NCORES)))
    return np.stack([np.asarray(res.results[b]["out"], dtype=np.float32) for b in range(4)], axis=0)
```
